# Optimizing a Trainium2 kernel written in Bass

```python
import math
import jax, jax.numpy as jnp
from jax import lax
import numpy as np

D_MODEL = 1024
BATCH = 4
SEQ = 8192
DEPTH = 4

MIX_WIDTH = D_MODEL
PLE_DIM = 256
QBLOCK = 128
EPS = 1e-6
W_A = MIX_WIDTH // 4
POOL_WINDOWS = (2, 4, 8, 16)
N_POOL_GROUPS = 4
POOL_GC = W_A // N_POOL_GROUPS
W_B = MIX_WIDTH - W_A
DV_B = 128
H_B = W_B // DV_B
DK_B = DV_B // 2
B_ROT = DK_B // 4
ROPE_THETA = 500000.0
W_C = MIX_WIDTH // 2
DV_C = 128
H_C = W_C // DV_C
D_NOPE = 128
D_ROPE = 64
DQK_C = D_NOPE + D_ROPE
Q_LORA = 256
KV_LORA = 128
MLA_THETA = 10000.0
W_D = MIX_WIDTH - W_C
DK_D = 128
DV_D = 128
H_D = W_D // DV_D
CONV_K = 4
CHUNK = 64
CONV_CH = H_D * (2 * DK_D + DV_D)
EVEN_WIDTHS = (W_A, W_A, 2 * H_B * DK_B, 2 * H_B * DK_B, H_B * DV_B, W_B)
ODD_WIDTHS = (Q_LORA, KV_LORA, D_ROPE, W_C, CONV_CH, H_D, H_D, W_D)

kernel_name = "hybrid_pool_diffattn_mla_gdn_trunk"


def _split(z, widths):
    idx = np.cumsum(widths)[:-1].tolist()
    return jnp.split(z, idx, axis=-1)


def rmsnorm(x, g):
    xf = x.astype(jnp.float32)
    y = xf * lax.rsqrt(jnp.mean(xf * xf, axis=-1, keepdims=True) + EPS)
    return (y * g.astype(jnp.float32)).astype(x.dtype)


def l2norm(x):
    xf = x.astype(jnp.float32)
    return xf * lax.rsqrt(jnp.sum(xf * xf, axis=-1, keepdims=True) + EPS)


def rope(x, positions, theta):
    r = x.shape[-1]
    half = r // 2
    inv = jnp.power(jnp.float32(theta), -jnp.arange(half, dtype=jnp.float32) * (2.0 / r))
    ang = positions.astype(jnp.float32)[..., None] * inv
    ang = ang.reshape(ang.shape[:2] + (1,) * (x.ndim - 3) + (half,))
    c, s = jnp.cos(ang), jnp.sin(ang)
    xf = x.astype(jnp.float32)
    x1, x2 = xf[..., :half], xf[..., half:]
    return jnp.concatenate([x1 * c - x2 * s, x2 * c + x1 * s], axis=-1).astype(x.dtype)


def partial_rope(x, positions, rot_dim, theta):
    return jnp.concatenate([rope(x[..., :rot_dim], positions, theta), x[..., rot_dim:]], axis=-1)


def _sweep_query_blocks(block_fn, q):
    B, T = q.shape[:2]
    nb = T // QBLOCK
    qb = jnp.moveaxis(q.reshape((B, nb, QBLOCK) + q.shape[2:]), 1, 0)
    starts = jnp.arange(nb, dtype=jnp.int32) * QBLOCK
    out = lax.map(block_fn, (qb, starts))
    return jnp.moveaxis(out, 0, 1).reshape((B, T) + out.shape[3:])


def multiscale_pool(x):
    B, T, _ = x.shape
    xf = x.astype(jnp.float32).reshape(B, T, N_POOL_GROUPS, POOL_GC)
    S = jnp.pad(jnp.cumsum(xf, axis=1), ((0, 0), (1, 0), (0, 0), (0, 0)))
    t = jnp.arange(T)
    outs = []
    for gi, w in enumerate(POOL_WINDOWS):
        upper = S[:, 1:, gi]
        lower = jnp.pad(S[:, :T + 1 - w, gi], ((0, 0), (w - 1, 0), (0, 0)))
        cnt = jnp.minimum(t + 1, w).astype(jnp.float32)[None, :, None]
        outs.append((upper - lower) / cnt - xf[:, :, gi])
    return jnp.stack(outs, axis=2).astype(x.dtype)


def diff_attention(q, k, v, lam):
    T = k.shape[1]
    scale = q.shape[-1] ** -0.5
    kpos = jnp.arange(T)

    def block(args):
        qb, start = args
        s = jnp.einsum('bqhmd,bkhmd->bhmqk', qb, k).astype(jnp.float32) * scale
        mask = kpos[None, :] <= (start + jnp.arange(QBLOCK))[:, None]
        pr = jax.nn.softmax(jnp.where(mask, s, -jnp.inf), axis=-1)
        a = pr[:, :, 0] - lam * pr[:, :, 1]
        return jnp.einsum('bhqk,bkhe->bqhe', a.astype(v.dtype), v)

    return _sweep_query_blocks(block, q)


def causal_attention(q, k, v):
    T = k.shape[1]
    scale = q.shape[-1] ** -0.5
    kpos = jnp.arange(T)

    def block(args):
        qb, start = args
        s = jnp.einsum('bqhd,bkhd->bhqk', qb, k).astype(jnp.float32) * scale
        mask = kpos[None, :] <= (start + jnp.arange(QBLOCK))[:, None]
        pr = jax.nn.softmax(jnp.where(mask, s, -jnp.inf), axis=-1)
        return jnp.einsum('bhqk,bkhe->bqhe', pr.astype(v.dtype), v)

    return _sweep_query_blocks(block, q)


def causal_depthwise_conv(x, w):
    K, C = w.shape
    return lax.conv_general_dilated(
        x, w[:, None, :].astype(x.dtype), window_strides=(1,), padding=[(K - 1, 0)],
        dimension_numbers=('NWC', 'WIO', 'NWC'), feature_group_count=C)


def gated_delta_rule(q, k, v, g, beta):
    B, T, H, DK = q.shape
    DV = v.shape[-1]
    N = T // CHUNK
    f32 = jnp.float32

    def to_chunks(a):
        a = a.astype(f32).reshape((B, N, CHUNK, H) + a.shape[3:])
        return jnp.moveaxis(a, 3, 1)

    q, k, v, g, beta = (to_chunks(a) for a in (q, k, v, g, beta))
    gc = jnp.cumsum(g, axis=-1)
    i = jnp.arange(CHUNK)
    incl = i[:, None] >= i[None, :]
    strict = i[:, None] > i[None, :]
    gamma = jnp.exp(jnp.where(incl, gc[..., :, None] - gc[..., None, :], -jnp.inf))
    kk = jnp.einsum('bhncd,bhnsd->bhncs', k, k)
    a_mat = jnp.eye(CHUNK, dtype=f32) + jnp.where(strict, kk * gamma * beta[..., :, None], 0.0)
    rhs = jnp.concatenate([v * beta[..., None], k * (beta * jnp.exp(gc))[..., None]], axis=-1)
    sol = lax.linalg.triangular_solve(a_mat, rhs, left_side=True, lower=True, unit_diagonal=True)
    u, w = sol[..., :DV], sol[..., DV:]
    qk = jnp.einsum('bhncd,bhnsd->bhncs', q, k) * gamma
    q_dec = q * jnp.exp(gc)[..., None]
    k_dec = k * jnp.exp(gc[..., -1:] - gc)[..., None]
    g_last = jnp.exp(gc[..., -1])

    def step(S, inp):
        qk_c, qd_c, kd_c, u_c, w_c, gl_c = inp
        v_new = u_c - jnp.einsum('bhcd,bhde->bhce', w_c, S)
        o = jnp.einsum('bhcd,bhde->bhce', qd_c, S) + jnp.einsum('bhcs,bhse->bhce', qk_c, v_new)
        S = S * gl_c[..., None, None] + jnp.einsum('bhcd,bhce->bhde', kd_c, v_new)
        return S, o

    xs = tuple(jnp.moveaxis(a, 2, 0) for a in (qk, q_dec, k_dec, u, w, g_last))
    _, o = lax.scan(step, jnp.zeros((B, H, DK, DV), f32), xs)
    o = jnp.moveaxis(o, 0, 2)
    return jnp.moveaxis(o, 1, 3).reshape(B, T, H, DV)


def even_layer(h, layer_idx, positions, w_in, pool_w, pool_scale, q_norm, k_norm,
               lam_vec, subln, w_out):
    B, T, _ = h.shape
    z = h @ w_in
    a_in, a_gate, q, k, v, b_gate = _split(z, EVEN_WIDTHS)
    a = multiscale_pool(a_in)
    a = jnp.einsum('btgc,gce->btge', a, pool_w).reshape(B, T, W_A) * pool_scale
    a = a * jax.nn.silu(a_gate)
    q = q.reshape(B, T, H_B, 2, DK_B)
    k = k.reshape(B, T, H_B, 2, DK_B)
    v = v.reshape(B, T, H_B, DV_B)
    q = partial_rope(rmsnorm(q, q_norm), positions, B_ROT, ROPE_THETA)
    k = partial_rope(rmsnorm(k, k_norm), positions, B_ROT, ROPE_THETA)
    lam_init = 0.8 - 0.6 * math.exp(-0.3 * layer_idx)
    lv = lam_vec.astype(jnp.float32)
    lam = jnp.exp(jnp.sum(lv[0] * lv[1])) - jnp.exp(jnp.sum(lv[2] * lv[3])) + lam_init
    o = diff_attention(q, k, v, lam)
    o = rmsnorm(o, subln) * (1.0 - lam_init)
    o = o.reshape(B, T, W_B) * jax.nn.silu(b_gate)
    return jnp.concatenate([a, o], axis=-1) @ w_out


def odd_layer(h, positions, w_in, q_a_norm, w_uq, kv_a_norm, w_ukv, q_norm, k_norm,
              conv_w, a_log, dt_bias, o_norm, w_out):
    B, T, _ = h.shape
    z = h @ w_in
    cq, ckv, k_rope, c_gate, qkv, d_b, d_a, d_gate = _split(z, ODD_WIDTHS)
    q = (rmsnorm(cq, q_a_norm) @ w_uq).reshape(B, T, H_C, DQK_C)
    kv = (rmsnorm(ckv, kv_a_norm) @ w_ukv).reshape(B, T, H_C, D_NOPE + DV_C)
    k_nope, v_c = kv[..., :D_NOPE], kv[..., D_NOPE:]
    k = jnp.concatenate([k_nope, jnp.broadcast_to(k_rope[:, :, None, :], (B, T, H_C, D_ROPE))], axis=-1)
    q = rmsnorm(q, q_norm)
    k = rmsnorm(k, k_norm)
    q = jnp.concatenate([q[..., :D_NOPE], rope(q[..., D_NOPE:], positions, MLA_THETA)], axis=-1)
    k = jnp.concatenate([k[..., :D_NOPE], rope(k[..., D_NOPE:], positions, MLA_THETA)], axis=-1)
    oc = causal_attention(q, k, v_c).reshape(B, T, W_C) * jax.nn.silu(c_gate)
    qkv = jax.nn.silu(causal_depthwise_conv(qkv, conv_w))
    dq = qkv[..., :H_D * DK_D].reshape(B, T, H_D, DK_D)
    dk = qkv[..., H_D * DK_D:2 * H_D * DK_D].reshape(B, T, H_D, DK_D)
    dv = qkv[..., 2 * H_D * DK_D:].reshape(B, T, H_D, DV_D)
    dq = l2norm(dq) * (DK_D ** -0.5)
    dk = l2norm(dk)
    beta = jax.nn.sigmoid(d_b.astype(jnp.float32))
    g = -jnp.exp(a_log.astype(jnp.float32)) * jax.nn.softplus(
        d_a.astype(jnp.float32) + dt_bias.astype(jnp.float32))
    od = gated_delta_rule(dq, dk, dv, g, beta).astype(h.dtype)
    od = rmsnorm(od, o_norm).reshape(B, T, W_D) * jax.nn.silu(d_gate)
    return jnp.concatenate([oc, od], axis=-1) @ w_out


def setup_inputs(seed: int = 0) -> dict:
    key = jax.random.key(seed)
    ks = jax.random.split(key, 32)
    f32 = jnp.float32
    n_even = (DEPTH + 1) // 2
    n_odd = DEPTH // 2

    def nrm(k, shape, fan_in):
        return jax.random.normal(k, shape, f32) * (fan_in ** -0.5)

    def gain(k, shape):
        return 1.0 + 0.02 * jax.random.normal(k, shape, f32)

    x = jax.random.normal(ks[0], (BATCH, SEQ, D_MODEL), f32)
    p = jax.random.normal(ks[1], (DEPTH, BATCH, SEQ, PLE_DIM), f32)
    positions = jnp.broadcast_to(jnp.arange(SEQ, dtype=jnp.int32), (BATCH, SEQ))
    dt = jnp.exp(jax.random.uniform(ks[24], (n_odd, H_D), f32, math.log(1e-3), math.log(1e-1)))
    return {
        "x": x,
        "p": p,
        "positions": positions,
        "norm_g": gain(ks[2], (DEPTH, D_MODEL)),
        "ple_w_gate": nrm(ks[3], (DEPTH, D_MODEL, D_MODEL), D_MODEL),
        "ple_w_proj": nrm(ks[4], (DEPTH, PLE_DIM, D_MODEL), PLE_DIM),
        "ev_w_in": nrm(ks[5], (n_even, D_MODEL, sum(EVEN_WIDTHS)), D_MODEL),
        "ev_pool_w": nrm(ks[6], (n_even, N_POOL_GROUPS, POOL_GC, POOL_GC), POOL_GC),
        "ev_pool_scale": gain(ks[7], (n_even, W_A)),
        "ev_q_norm": gain(ks[8], (n_even, DK_B)),
        "ev_k_norm": gain(ks[9], (n_even, DK_B)),
        "ev_lambda": 0.1 * jax.random.normal(ks[10], (n_even, 4, DK_B), f32),
        "ev_subln": gain(ks[11], (n_even, DV_B)),
        "ev_w_out": nrm(ks[12], (n_even, MIX_WIDTH, D_MODEL), MIX_WIDTH),
        "od_w_in": nrm(ks[13], (n_odd, D_MODEL, sum(ODD_WIDTHS)), D_MODEL),
        "od_q_a_norm": gain(ks[14], (n_odd, Q_LORA)),
        "od_w_uq": nrm(ks[15], (n_odd, Q_LORA, H_C * DQK_C), Q_LORA),
        "od_kv_a_norm": gain(ks[16], (n_odd, KV_LORA)),
        "od_w_ukv": nrm(ks[17], (n_odd, KV_LORA, H_C * (D_NOPE + DV_C)), KV_LORA),
        "od_q_norm": gain(ks[18], (n_odd, DQK_C)),
        "od_k_norm": gain(ks[19], (n_odd, DQK_C)),
        "od_conv_w": nrm(ks[20], (n_odd, CONV_K, CONV_CH), CONV_K),
        "od_a_log": jnp.log(jax.random.uniform(ks[21], (n_odd, H_D), f32, 1.0, 16.0)),
        "od_dt_bias": dt + jnp.log(-jnp.expm1(-dt)),
        "od_o_norm": gain(ks[22], (n_odd, DV_D)),
        "od_w_out": nrm(ks[23], (n_odd, MIX_WIDTH, D_MODEL), MIX_WIDTH),
    }


def reference(x, p, positions, norm_g, ple_w_gate, ple_w_proj,
              ev_w_in, ev_pool_w, ev_pool_scale, ev_q_norm, ev_k_norm, ev_lambda, ev_subln, ev_w_out,
              od_w_in, od_q_a_norm, od_w_uq, od_kv_a_norm, od_w_ukv, od_q_norm, od_k_norm,
              od_conv_w, od_a_log, od_dt_bias, od_o_norm, od_w_out):
    h = x
    for i in range(DEPTH):
        hn = rmsnorm(h, norm_g[i])
        j = i // 2
        if i % 2 == 0:
            m = even_layer(hn, i, positions, ev_w_in[j], ev_pool_w[j], ev_pool_scale[j],
                           ev_q_norm[j], ev_k_norm[j], ev_lambda[j], ev_subln[j], ev_w_out[j])
        else:
            m = odd_layer(hn, positions, od_w_in[j], od_q_a_norm[j], od_w_uq[j], od_kv_a_norm[j],
                          od_w_ukv[j], od_q_norm[j], od_k_norm[j], od_conv_w[j], od_a_log[j],
                          od_dt_bias[j], od_o_norm[j], od_w_out[j])
        h = h + m
        h = h + jax.nn.sigmoid(h @ ple_w_gate[i]) * (p[i] @ ple_w_proj[i])
    return h
```

```python
import math
from contextlib import ExitStack
import numpy as np
import concourse.bass as bass
import concourse.mybir as mybir
from concourse.bass_utils import run_bass_kernel_spmd

F32 = mybir.dt.float32
BF16 = mybir.dt.bfloat16
I32 = mybir.dt.int32
AF = mybir.ActivationFunctionType
ALU = mybir.AluOpType
AX = mybir.AxisListType

ENGS = ('pe', 'act', 'dve', 'pool', 'sp')
NSLOT = 8
import os
SELF_WAIT = bool(int(os.environ.get('SELF_WAIT', '1')))


class Buf:
    __slots__ = ('name', 'w', 'r', 'excl')

    def __init__(self, name='', excl=False):
        self.name = name
        self.w = None
        self.r = []
        self.excl = excl


class Op:
    __slots__ = ('eng', 'fn', 'waits', 'sig', 'sigidx', 'seq', 'is_dma', 'slot', 'dval', 'pre')

    def __init__(self, eng, fn, is_dma):
        self.eng = eng
        self.fn = fn
        self.is_dma = is_dma
        self.waits = []
        self.sig = False
        self.sigidx = None
        self.slot = None
        self.dval = None
        self.pre = None


class Prog:
    def __init__(self, nc):
        self.nc = nc
        self.es = ExitStack()
        self.streams = {e: [] for e in ENGS}
        self.known = {e: {f: -1 for f in ENGS} for e in ENGS}
        self.kdma = {e: {} for e in ENGS}
        self.ndma = {e: 0 for e in ENGS}
        self.slot_last = {}
        self.nsb = 0
        self.pending = {e: [] for e in ENGS}
        self.cuts = []

    def sbuf(self, name, shape, dt):
        return self.es.enter_context(self.nc.sbuf_tensor(name, list(shape), dt))

    def psum(self, name, shape, dt):
        return self.es.enter_context(self.nc.psum_tensor(name, list(shape), dt))

    def _dep(self, op, d):
        e = op.eng
        if d.is_dma:
            key = (d.eng, d.slot)
            if self.kdma[e].get(key, 0) >= d.dval:
                return
            self.kdma[e][key] = d.dval
            op.waits.append(d)
        else:
            if d.eng == e and not op.is_dma and (e == 'pe' or not SELF_WAIT):
                return
            if self.known[e][d.eng] >= d.seq:
                return
            self.known[e][d.eng] = d.seq
            d.sig = True
            op.waits.append(d)

    def _record(self, op, reads, writes):
        e = op.eng
        st = self.streams[e]
        op.seq = len(st)
        if self.pending[e]:
            for d in self.pending[e]:
                if d.eng != e or d.is_dma:
                    self._dep(op, d)
            self.pending[e] = []
        for b in reads:
            if b.w is not None:
                self._dep(op, b.w)
            if b.excl:
                for r in b.r:
                    if r.eng != e:
                        self._dep(op, r)
        for b in writes:
            if b.w is not None and (b.w.eng != e or b.w.is_dma or op.is_dma):
                self._dep(op, b.w)
            for r in b.r:
                if r.eng != e or r.is_dma or op.is_dma:
                    self._dep(op, r)
        for b in reads:
            b.r.append(op)
        for b in writes:
            b.w = op
            b.r = []
        st.append(op)
        return op

    def op(self, eng, fn, reads=(), writes=()):
        return self._record(Op(eng, fn, False), reads, writes)

    def cut(self):
        self.barrier()
        self.cuts.append({e: len(self.streams[e]) for e in ENGS})

    def barrier(self):
        lasts = [st[-1] for st in self.streams.values() if st and not st[-1].is_dma]
        for st in self.streams.values():
            for op in reversed(st):
                if not op.is_dma:
                    if op not in lasts:
                        lasts.append(op)
                    break
        dmas = list(self.slot_last.values())
        for e in ENGS:
            self.pending[e] = lasts + dmas

    def dma(self, q, out, in_, reads=(), writes=(), final=False, **kw):
        op = Op(q, None, True)
        j = self.ndma[q]
        self.ndma[q] = j + 1
        op.slot = j % NSLOT
        op.dval = 16 * (j // NSLOT + 1)
        prev = self.slot_last.get((q, op.slot))
        self.slot_last[(q, op.slot)] = op
        op.fn = lambda eng: eng.dma_start(out=out, in_=in_, **kw)
        if prev is not None:
            self._dep(op, prev)
        return self._record(op, reads, writes)

    def emit(self):
        nc = self.nc
        es = self.es
        sem = {e: es.enter_context(nc.semaphore("s_" + e)) for e in ENGS}
        dsem = {}
        for q in ENGS:
            if self.ndma[q]:
                for s in range(min(NSLOT, self.ndma[q])):
                    dsem[(q, s)] = es.enter_context(nc.semaphore("d_%s%d" % (q, s)))
        for e in ENGS:
            n = 0
            for op in self.streams[e]:
                if op.sig and not op.is_dma:
                    n += 1
                    op.sigidx = n
        streams = self.streams
        slot_last = self.slot_last

        def run(e, eng, lo, hi, last):
            for op in streams[e][lo:hi]:
                for d in op.waits:
                    if d.is_dma:
                        eng.wait_ge(dsem[(d.eng, d.slot)], d.dval)
                    else:
                        eng.wait_ge(sem[d.eng], d.sigidx)
                ins = op.fn(eng)
                if op.is_dma:
                    ins.then_inc(dsem[(e, op.slot)], 16)
                elif op.sig:
                    ins.then_inc(sem[e], 1)
            if last:
                for (q, s), op in slot_last.items():
                    if q == e:
                        eng.wait_ge(dsem[(q, s)], op.dval)

        bounds = self.cuts + [{e: len(streams[e]) for e in ENGS}]
        prev = {e: 0 for e in ENGS}
        for bi, b in enumerate(bounds):
            last = bi == len(bounds) - 1
            with nc.Block() as block:
                for e, deco in (('sp', block.sync), ('pe', block.tensor), ('act', block.scalar), ('dve', block.vector),
                                ('pool', block.gpsimd)):
                    lo, hi = prev[e], b[e]
                    if hi > lo or (last and any(q == e for (q, _s) in slot_last)):
                        deco(lambda eng, e=e, lo=lo, hi=hi, last=last: run(e, eng, lo, hi, last))
            prev = b
        es.close()


D = 1024
EPS = 1e-6
POOL_W = (2, 4, 8, 16)
ROPE_THETA = 500000.0
MLA_THETA = 10000.0
NEG = -30000.0
ARENA = 207 * 1024
VERBOSE = bool(int(os.environ.get('KVERBOSE', '0')))
SKIP = set(os.environ.get('KSKIP', '').split(','))
GSTAGE = int(os.environ.get('KGSTAGE', '99'))
GSUB = int(os.environ.get('KGSUB', '99'))


class TL:
    __slots__ = ('t', 'b')

    def __init__(self, t, name=''):
        self.t = t
        self.b = Buf(name)


class Ring:
    def __init__(self, items):
        self.items = items
        self.i = 0

    def next(self):
        x = self.items[self.i % len(self.items)]
        self.i += 1
        return x


def host_consts():
    c = {}
    c['ident'] = np.eye(128, dtype=np.float32)
    c['ones'] = np.ones((128, 128), dtype=np.float32)
    k = np.arange(128)[:, None]
    q = np.arange(512)[None, :]
    c['cmask'] = np.stack([np.where((128 * j + k) <= q, 0.0, NEG).astype(np.float32) for j in range(4)], axis=1)
    tp = np.arange(128)[:, None]
    t = np.arange(128)[None, :]
    cur, prv, fst = [], [], []
    for w in POOL_W:
        inwin = ((tp <= t) & (tp > t - w)).astype(np.float32)
        cur.append(inwin / w - np.eye(128, dtype=np.float32))
        prv.append(((tp - 128) > (t - w)).astype(np.float32) / w)
        cnt = np.minimum(t + 1, w).astype(np.float32)
        fst.append(inwin / cnt - np.eye(128, dtype=np.float32))
    c['bcur'] = np.stack(cur, axis=1).astype(np.float32)
    c['bprv'] = np.stack(prv, axis=1).astype(np.float32)
    c['bfst'] = np.stack(fst, axis=1).astype(np.float32)
    inv_e = np.power(np.float32(ROPE_THETA), -np.arange(8, dtype=np.float32) * np.float32(2.0 / 16)).astype(np.float32)
    inv_m = np.power(np.float32(MLA_THETA), -np.arange(32, dtype=np.float32) * np.float32(2.0 / 64)).astype(np.float32)
    c['invf'] = np.broadcast_to(np.concatenate([inv_e, inv_m])[None, :], (128, 40)).astype(np.float32).copy()
    s = np.arange(64)[:, None]
    cc = np.arange(64)[None, :]
    g = np.zeros((64, 4, 64), np.float32)
    g[:, 0, :] = (s <= cc)
    g[:, 1, :] = np.where(s >= cc, 0.0, NEG)
    g[:, 2, :] = np.where(cc >= s, 0.0, NEG)
    g[:, 3, :] = (s > cc)
    c['gdnc'] = g
    return c


class Core:
    def __init__(self, T, layers, dbg=()):
        self.T = T
        self.NT = T // 128
        self.layers = layers
        self.dbg = set(dbg)
        nc = self.nc = bass.Bass("TRN2", target_bir_lowering=False)
        self.P = Prog(nc)
        self.dbufs = {}
        self.cnt = 0
        self.arena = self.P.sbuf("arena", [128, ARENA], mybir.dt.uint8)
        self.off = 0
        self.maxoff = 0

    def din(self, name, shape, dt=F32):
        return self.nc.dram_tensor(name, list(shape), dt, kind="ExternalInput").ap()

    def dscr(self, name, shape, dt):
        kind = "ExternalOutput" if name in self.dbg else "Internal"
        return self.nc.dram_tensor(name, list(shape), dt, kind=kind).ap()

    def db(self, name, i):
        k = (name, i)
        if k not in self.dbufs:
            self.dbufs[k] = Buf("%s%s" % (name, i))
        return self.dbufs[k]

    def sb(self, name, shape, dt=F32):
        esz = 2 if dt == BF16 else 4
        n = 1
        for d in shape[1:]:
            n *= d
        nb = (n * esz + 31) // 32 * 32
        off = self.off
        self.off += nb
        assert self.off <= ARENA, ("SBUF arena overflow", name, self.off)
        self.maxoff = max(self.maxoff, self.off)
        v = self.arena[0:shape[0], off:off + n * esz].bitcast(dt)
        if len(shape) == 3:
            v = v.rearrange("p (a b) -> p a b", b=shape[2])
        elif len(shape) != 2:
            raise ValueError(shape)
        return TL(v, name)

    def phase_reset(self):
        self.P.barrier()
        if VERBOSE:
            print("phase end: off", self.off, "persist", self.persist_top)
        self.off = self.persist_top

    def ring(self, name, shape, dt, n):
        return Ring([self.sb("%s%d" % (name, i), shape, dt) for i in range(n)])

    def A(self, eng, meth, reads, writes, *a, **kw):
        rb = [x.b if isinstance(x, TL) else x for x in reads]
        wb = [x.b if isinstance(x, TL) else x for x in writes]
        return self.P.op(eng, lambda e: getattr(e, meth)(*a, **kw), rb, wb)

    def dma(self, q, out, in_, reads, writes, **kw):
        rb = [x.b if isinstance(x, TL) else x for x in reads]
        wb = [x.b if isinstance(x, TL) else x for x in writes]
        return self.P.dma(q, out, in_, rb, wb, **kw)

    def rstd(self, ss, n, tmp, out, eng2='act'):
        A = self.A
        A('dve', 'tensor_scalar', [ss[0]], [tmp[0]], out=tmp[1], in0=ss[1], scalar1=1.0 / n, scalar2=EPS,
          op0=ALU.mult, op1=ALU.add)
        A('dve', 'reciprocal', [tmp[0]], [tmp[0]], out=tmp[1], in_=tmp[1])
        A('act', 'activation', [tmp[0]], [out[0]], out=out[1], in_=tmp[1], func=AF.Sqrt)

    def setup(self):
        T, NT, nc = self.T, self.NT, self.nc
        A, dma, sb = self.A, self.dma, self.sb
        self.x = self.din("x", [T, D])
        self.y = self.nc.dram_tensor("y", [T, D], F32, kind="ExternalOutput").ap()
        self.p = self.din("p", [4, T, 256])
        self.posd = self.din("pos", [128, NT], I32)
        self.H = self.dscr("H", [T, D], F32)
        self.MIXT = self.dscr("MIXT", [D, T], BF16)
        hc = host_consts()
        cd = {k: self.din("c_" + k, v.shape) for k, v in hc.items()}
        self.PS = [TL(self.P.psum("ps%d" % i, [128, 512], F32), "ps%d" % i) for i in range(8)]
        for ps in self.PS:
            ps.b.excl = True
        self.identf = sb("identf", [128, 128])
        self.identb = sb("identb", [128, 128], BF16)
        self.onesf = sb("onesf", [128, 128])
        self.onesb = sb("onesb", [128, 128], BF16)
        self.cmask = sb("cmask", [128, 4, 512], BF16)
        self.bcur = sb("bcur", [128, 4, 128])
        self.bprv = sb("bprv", [128, 4, 128])
        self.bfst = sb("bfst", [128, 4, 128])
        self.gdnc = sb("gdnc", [64, 4, 64])
        invf = sb("invf", [128, 40])
        dma('sp', self.identf.t[:], cd['ident'], [], [self.identf])
        dma('sp', self.onesf.t[:], cd['ones'], [], [self.onesf])
        dma('pool', self.identb.t[:], cd['ident'], [], [self.identb])
        dma('pool', self.onesb.t[:], cd['ones'], [], [self.onesb])
        dma('pool', self.cmask.t[:], cd['cmask'], [], [self.cmask])
        dma('sp', self.bcur.t[:], cd['bcur'], [], [self.bcur])
        dma('sp', self.bprv.t[:], cd['bprv'], [], [self.bprv])
        dma('sp', self.bfst.t[:], cd['bfst'], [], [self.bfst])
        dma('sp', self.gdnc.t[:], cd['gdnc'], [], [self.gdnc])
        dma('sp', invf.t[:], cd['invf'], [], [invf])
        self.ROPE = self.dscr("ROPE", [128, NT, 80], F32)
        self.rp = self.ring("rp", [128, 80], F32, 2)
        self.hs = self.ring("hs", [128, D], F32, 2)
        self.junk = sb("junk", [128, D])
        self.xnb = self.ring("xnb", [128, D], BF16, 2)
        self.xnT = self.ring("xnT", [128, 8, 128], BF16, 2)
        self.small = self.ring("small", [128, 16], F32, 8)
        self.lam2 = sb("lam2", [128, 8])
        self.gnorm = {}
        for li in self.layers:
            g = sb("gn%d" % li, [128, 8])
            dma('sp', g.t[:], self.din("l%d_gn" % li, [128, 8]), [], [g])
            self.gnorm[li] = g
        self.persist_top = self.off
        self.cs = sb("cs", [128, NT, 40])
        self.sn = sb("sn", [128, NT, 40])
        posi = sb("posi", [128, NT], I32)
        posf = sb("posf", [128, NT])
        dma('sp', posi.t[:], self.posd, [], [posi])
        A('dve', 'tensor_copy', [posi], [posf], out=posf.t[:], in_=posi.t[:])
        ang = sb("ang", [128, NT, 40])
        for t in range(NT):
            A('dve', 'tensor_scalar', [invf, posf], [ang], out=ang.t[:, t, :], in0=invf.t[:], scalar1=posf.t[:, t:t + 1],
              scalar2=None, op0=ALU.mult)
        TWO_PI = 2.0 * math.pi
        HI = 6.28125
        LO = TWO_PI - HI
        tmpa = sb("tmpa", [128, NT, 40])
        ki = sb("ki", [128, NT, 40], I32)
        kf = sb("kf", [128, NT, 40])
        hpi = sb("hpi", [128, 1])
        A('dve', 'memset', [], [hpi], hpi.t[:], 0.5 * math.pi)
        A('dve', 'tensor_scalar', [ang], [tmpa], out=tmpa.t[:], in0=ang.t[:], scalar1=1.0 / TWO_PI, scalar2=None, op0=ALU.mult)
        A('dve', 'tensor_copy', [tmpa], [ki], out=ki.t[:], in_=tmpa.t[:])
        A('dve', 'tensor_copy', [ki], [kf], out=kf.t[:], in_=ki.t[:])
        A('dve', 'scalar_tensor_tensor', [kf, ang], [tmpa], out=tmpa.t[:], in0=kf.t[:], scalar=-HI, in1=ang.t[:], op0=ALU.mult, op1=ALU.add)
        A('dve', 'scalar_tensor_tensor', [kf, tmpa], [tmpa], out=tmpa.t[:], in0=kf.t[:], scalar=-LO, in1=tmpa.t[:], op0=ALU.mult, op1=ALU.add)
        s2 = kf
        A('act', 'activation', [tmpa], [s2], out=s2.t[:], in_=tmpa.t[:], func=AF.Sin, scale=0.5)
        A('act', 'activation', [tmpa], [ang], out=ang.t[:], in_=tmpa.t[:], func=AF.Abs)
        A('act', 'activation', [ang, hpi], [tmpa], out=tmpa.t[:], in_=ang.t[:], func=AF.Sin, scale=-0.5, bias=hpi.t[:, 0:1])
        A('dve', 'scalar_tensor_tensor', [s2, tmpa], [self.sn], out=self.sn.t[:], in0=s2.t[:], scalar=2.0, in1=tmpa.t[:], op0=ALU.mult, op1=ALU.mult)
        A('dve', 'tensor_tensor', [s2], [ang], out=ang.t[:], in0=s2.t[:], in1=s2.t[:], op=ALU.mult)
        A('dve', 'tensor_scalar', [ang], [self.cs], out=self.cs.t[:], in0=ang.t[:], scalar1=-2.0, scalar2=1.0, op0=ALU.mult, op1=ALU.add)
        dma('sp', self.ROPE[:, :, 0:40], self.cs.t[:], [self.cs], [self.db('ROPE', 0)])
        dma('sp', self.ROPE[:, :, 40:80], self.sn.t[:], [self.sn], [self.db('ROPE', 1)])

    def rope_tile(self, t):
        rp = self.rp.next()
        self.dma('sp', rp.t[:], self.ROPE[:, t, :], [self.db('ROPE', 0), self.db('ROPE', 1)], [rp])
        return rp

    def norm_transpose(self, hs, g):
        A = self.A
        sm = self.small.next()
        A('act', 'activation', [hs], [self.junk, sm], out=self.junk.t[:], in_=hs.t[:], func=AF.Square,
          accum_out=sm.t[:, 0:1])
        self.rstd((sm, sm.t[:, 0:1]), D, (sm, sm.t[:, 1:2]), (sm, sm.t[:, 2:3]))
        xb = self.xnb.next()
        A('dve', 'tensor_scalar', [hs, sm], [xb], out=xb.t[:], in0=hs.t[:], scalar1=sm.t[:, 2:3], scalar2=None,
          op0=ALU.mult)
        return self.transpose_scale(xb, g, 8)

    def transpose_scale(self, xb, g, nk, out=None):
        A = self.A
        ps = self.psn()
        pv = ps.t[:, :].bitcast(BF16)
        for k in range(nk):
            A('pe', 'transpose', [xb, self.identb], [ps], out=pv[:, k * 128:(k + 1) * 128],
              in_=xb.t[:, k * 128:(k + 1) * 128], identity=self.identb.t[:])
        xT = out if out is not None else self.xnT.next()
        for k in range(nk):
            if g is None:
                if k % 2 == 0:
                    A('act', 'copy', [ps], [xT], out=xT.t[:, k, :], in_=pv[:, k * 128:(k + 1) * 128])
                else:
                    A('dve', 'tensor_copy', [ps], [xT], out=xT.t[:, k, :], in_=pv[:, k * 128:(k + 1) * 128])
            elif k % 2 == 0:
                A('act', 'activation', [ps, g], [xT], out=xT.t[:, k, :], in_=pv[:, k * 128:(k + 1) * 128],
                  func=AF.Copy, scale=g.t[:, k:k + 1])
            else:
                A('dve', 'tensor_scalar', [ps, g], [xT], out=xT.t[:, k, :], in0=pv[:, k * 128:(k + 1) * 128],
                  scalar1=g.t[:, k:k + 1], scalar2=None, op0=ALU.mult)
        return xT

    def psn(self):
        self.cnt += 1
        return self.PS[self.cnt % 8]


def _even_methods():
    def even_setup(self, j):
        T = self.T
        s = "e%d_" % j
        W = {}
        W['wtm'] = self.din(s + "wtm", [D, 2560])
        W['wfm'] = self.din(s + "wfm", [D, 1024])
        W['poolw'] = self.din(s + "poolw", [64, 4, 64])
        W['pscale'] = self.din(s + "pscale", [64, 4])
        W['qg'] = self.din(s + "qg", [128, 64])
        W['kg'] = self.din(s + "kg", [128, 64])
        W['lam'] = self.din(s + "lam", [128, 256])
        W['subln'] = self.din(s + "subln", [128, 1])
        return W

    def even_scratch(self):
        T = self.T
        if not hasattr(self, 'QT'):
            self.QT = self.dscr("QT", [6, 128, T], BF16)
            self.KT = self.dscr("KT", [6, 128, T], BF16)
        if not hasattr(self, 'V'):
            self.V = self.dscr("V", [T, 768], BF16)
            self.GT = self.dscr("GT", [768, T], BF16)
        self.wtm = self.sb("wtm", [128, 8, 2560], BF16)
        self.wfm = self.sb("wfm", [128, 8, 1024], BF16)
        self.poolw = self.sb("poolw", [64, 4, 64])
        self.pscale = self.sb("pscale", [64, 4])
        self.qg = self.sb("qg", [128, 64])
        self.kg = self.sb("kg", [128, 64])
        self.lamv = self.sb("lamv", [128, 256])
        self.subln = self.sb("subln", [128, 1])
        self.ain = self.ring("ain", [128, 256], F32, 2)
        self.qf = self.ring("qf", [128, 384], F32, 2)
        self.qsq = self.sb("qsq", [128, 384])
        self.qb = self.ring("qb", [128, 384], BF16, 2)
        self.rt = self.ring("rt", [128, 6, 8], F32, 8)
        self.qTs = self.ring("qTs", [128, 3, 128], BF16, 2)
        self.vb = self.ring("vb", [128, 384], BF16, 2)
        self.tmb = self.ring("tmb", [128, 512], BF16, 2)
        self.gt6 = self.ring("gt6", [128, 6, 128], BF16, 2)
        self.plT = self.ring("plT", [64, 128], F32, 2)
        self.agT = self.ring("agT", [64, 128], F32, 4)
        self.aoT = self.ring("aoT", [64, 128], BF16, 2)

    def phaseA_even(self, li, j, W, Hin):
        T, NT = self.T, self.NT
        A, dma = self.A, self.dma
        self.phase_reset()
        self.even_scratch()
        g = self.gnorm[li]
        dma('pool', self.wtm.t[:], W['wtm'].rearrange("(k p) n -> p k n", p=128), [], [self.wtm])
        dma('pool', self.wfm.t[:], W['wfm'].rearrange("(k p) n -> p k n", p=128), [], [self.wfm])
        for nm, t in (('poolw', self.poolw), ('pscale', self.pscale), ('qg', self.qg), ('kg', self.kg),
                      ('lam', self.lamv), ('subln', self.subln)):
            dma('sp', t.t[:], W[nm], [], [t])
        lam_init = 0.8 - 0.6 * math.exp(-0.3 * li)
        l2 = self.lam2
        lv = self.lamv
        pr = self.sb("lampr", [128, 128])
        A('dve', 'tensor_tensor', [lv], [pr], out=pr.t[:, 0:64], in0=lv.t[:, 0:64], in1=lv.t[:, 64:128], op=ALU.mult)
        A('dve', 'tensor_tensor', [lv], [pr], out=pr.t[:, 64:128], in0=lv.t[:, 128:192], in1=lv.t[:, 192:256], op=ALU.mult)
        A('dve', 'reduce_sum', [pr], [l2], out=l2.t[:, 0:2], in_=pr.t[:, :].rearrange("p (a b) -> p a b", b=64), axis=AX.X)
        A('act', 'activation', [l2], [l2], out=l2.t[:, 2:4], in_=l2.t[:, 0:2], func=AF.Exp)
        A('dve', 'tensor_tensor', [l2], [l2], out=l2.t[:, 4:5], in0=l2.t[:, 3:4], in1=l2.t[:, 2:3], op=ALU.subtract)
        A('dve', 'tensor_scalar', [l2], [l2], out=l2.t[:, 5:6], in0=l2.t[:, 4:5], scalar1=-lam_init, scalar2=None,
          op0=ALU.add)
        A('dve', 'tensor_scalar', [self.subln], [l2], out=l2.t[:, 6:7], in0=self.subln.t[:, 0:1],
          scalar1=1.0 - lam_init, scalar2=None, op0=ALU.mult)
        aprev = None
        for t in range(NT):
            tok = slice(t * 128, (t + 1) * 128)
            hs = self.hs.next()
            dma('sp', hs.t[:], Hin[tok, :], [self.db('H', t)], [hs])
            xT = self.norm_transpose(hs, g)
            rp = self.rope_tile(t)
            for gi in range(7):
                c0 = gi * 384
                n = 384 if gi < 6 else 256
                ps = self.psn()
                for k in range(8):
                    A('pe', 'matmul', [xT, self.wtm], [ps], ps.t[:, 0:n], lhsT=xT.t[:, k, :],
                      rhs=self.wtm.t[:, k, c0:c0 + n], start=(k == 0), stop=(k == 7))
                if gi < 4:
                    gain = self.qg if gi < 2 else self.kg
                    dst = self.QT if gi < 2 else self.KT
                    dname = 'QT' if gi < 2 else 'KT'
                    qf = self.qf.next()
                    sm = self.small.next()
                    A('act', 'activation', [ps], [self.qsq], out=self.qsq.t[:], in_=ps.t[:, 0:384], func=AF.Square)
                    A('dve', 'reduce_sum', [self.qsq], [sm], out=sm.t[:, 0:6],
                      in_=self.qsq.t[:, :].rearrange("p (a b) -> p a b", b=64), axis=AX.X)
                    self.rstd((sm, sm.t[:, 0:6]), 64, (sm, sm.t[:, 6:12]), (sm, sm.t[:, 6:12]))
                    A('dve', 'tensor_tensor', [ps, sm], [qf], out=qf.t[:, :].rearrange("p (a b) -> p a b", b=64),
                      in0=ps.t[:, 0:384].rearrange("p (a b) -> p a b", b=64),
                      in1=sm.t[:, 6:12].unsqueeze(2).broadcast_to([128, 6, 64]), op=ALU.mult)
                    A('pool', 'tensor_tensor', [qf, gain], [qf], out=qf.t[:, :].rearrange("p (a b) -> p a b", b=64),
                      in0=qf.t[:, :].rearrange("p (a b) -> p a b", b=64),
                      in1=gain.t[:, :].unsqueeze(1).broadcast_to([128, 6, 64]), op=ALU.mult)
                    qb = self.qb.next()
                    A('act', 'copy', [qf], [qb], out=qb.t[:], in_=qf.t[:])
                    q3 = qf.t[:, :].rearrange("p (a b) -> p a b", b=64)
                    o3 = qb.t[:, :].rearrange("p (a b) -> p a b", b=64)
                    cs = rp.t[:, 0:8].unsqueeze(1).broadcast_to([128, 6, 8])
                    sn = rp.t[:, 40:48].unsqueeze(1).broadcast_to([128, 6, 8])
                    t1, t2, t3, t4 = self.rt.next(), self.rt.next(), self.rt.next(), self.rt.next()
                    A('dve', 'tensor_tensor', [qf, rp], [t1], out=t1.t[:], in0=q3[:, :, 0:8], in1=cs, op=ALU.mult)
                    A('pool', 'tensor_tensor', [qf, rp], [t2], out=t2.t[:], in0=q3[:, :, 8:16], in1=sn, op=ALU.mult)
                    A('dve', 'tensor_tensor', [qf, rp], [t3], out=t3.t[:], in0=q3[:, :, 8:16], in1=cs, op=ALU.mult)
                    A('pool', 'tensor_tensor', [qf, rp], [t4], out=t4.t[:], in0=q3[:, :, 0:8], in1=sn, op=ALU.mult)
                    A('dve', 'tensor_tensor', [t1, t2, qb], [qb], out=o3[:, :, 0:8], in0=t1.t[:], in1=t2.t[:], op=ALU.subtract)
                    A('dve', 'tensor_tensor', [t3, t4, qb], [qb], out=o3[:, :, 8:16], in0=t3.t[:], in1=t4.t[:], op=ALU.add)
                    qT = self.transpose_scale(qb, None, 3, out=self.qTs.next())
                    h0 = (gi % 2) * 3
                    dma('sp', dst[h0:h0 + 3, :, tok].rearrange("h p t -> p h t"), qT.t[:], [qT],
                        [self.db(dname, (h0 + hh, t)) for hh in range(3)])
                elif gi < 6:
                    vb = self.vb.next()
                    A('act', 'copy', [ps], [vb], out=vb.t[:], in_=ps.t[:, 0:384])
                    c = (gi - 4) * 384
                    dma('sp', self.V[tok, c:c + 384], vb.t[:], [vb], [self.db('V', (h, t)) for h in range(6)])
                else:
                    acur = self.ain.next()
                    A('act', 'copy', [ps], [acur], out=acur.t[:], in_=ps.t[:, 0:256])
            agl = []
            g6 = self.gt6.next()
            for gi in range(2):
                ps = self.psn()
                for k in range(8):
                    A('pe', 'matmul', [xT, self.wfm], [ps], ps.t[:], lhsT=xT.t[:, k, :], rhs=self.wfm.t[:, k, gi * 512:(gi + 1) * 512],
                      start=(k == 0), stop=(k == 7))
                tb = self.tmb.next()
                A('act', 'activation', [ps], [tb], out=tb.t[:], in_=ps.t[:], func=AF.Silu)
                pst = self.psn()
                pv = pst.t[:, :].bitcast(BF16)
                if gi == 0:
                    for j in range(4):
                        A('pe', 'transpose', [tb, self.identb], [pst], out=pv[0:64, j * 128:(j + 1) * 128], in_=tb.t[:, j * 64:(j + 1) * 64],
                          identity=self.identb.t[:])
                    for j in range(2):
                        A('pe', 'transpose', [tb, self.identb], [pst], out=pv[:, 512 + j * 128:512 + (j + 1) * 128],
                          in_=tb.t[:, 256 + j * 128:256 + (j + 1) * 128], identity=self.identb.t[:])
                    for j in range(4):
                        ag = self.agT.next()
                        A('act', 'copy', [pst], [ag], out=ag.t[:], in_=pv[0:64, j * 128:(j + 1) * 128])
                        agl.append(ag)
                    A('dve', 'tensor_copy', [pst], [g6], out=g6.t[:, 0:2, :], in_=pv[:, 512:768].rearrange("p (a b) -> p a b", b=128))
                else:
                    for j in range(4):
                        A('pe', 'transpose', [tb, self.identb], [pst], out=pv[:, j * 128:(j + 1) * 128], in_=tb.t[:, j * 128:(j + 1) * 128],
                          identity=self.identb.t[:])
                    A('dve', 'tensor_copy', [pst], [g6], out=g6.t[:, 2:6, :], in_=pv[:, 0:512].rearrange("p (a b) -> p a b", b=128))
            dma('sp', self.GT[0:768, tok].rearrange("(h p) t -> p h t", p=128), g6.t[:], [g6], [self.db('GT', (hh, t)) for hh in range(6)])
            for gi in range(4):
                ps = self.psn()
                if t == 0:
                    A('pe', 'matmul', [acur, self.bfst], [ps], ps.t[0:64, 0:128], lhsT=acur.t[:, gi * 64:(gi + 1) * 64],
                      rhs=self.bfst.t[:, gi, :], start=True, stop=True)
                else:
                    A('pe', 'matmul', [acur, self.bcur], [ps], ps.t[0:64, 0:128], lhsT=acur.t[:, gi * 64:(gi + 1) * 64],
                      rhs=self.bcur.t[:, gi, :], start=True, stop=False)
                    A('pe', 'matmul', [aprev, self.bprv], [ps], ps.t[0:64, 0:128], lhsT=aprev.t[:, gi * 64:(gi + 1) * 64],
                      rhs=self.bprv.t[:, gi, :], start=False, stop=True)
                pl = self.plT.next()
                A('dve', 'tensor_copy', [ps], [pl], out=pl.t[:], in_=ps.t[0:64, 0:128])
                ps2 = self.psn()
                A('pe', 'matmul', [pl, self.poolw], [ps2], ps2.t[0:64, 0:128], lhsT=self.poolw.t[:, gi, :], rhs=pl.t[:],
                  start=True, stop=True)
                ao = self.aoT.next()
                A('dve', 'scalar_tensor_tensor', [ps2, self.pscale, agl[gi]], [ao], out=ao.t[:], in0=ps2.t[0:64, 0:128],
                  scalar=self.pscale.t[:, gi:gi + 1], in1=agl[gi].t[:], op0=ALU.mult, op1=ALU.mult)
                dma('sp', self.MIXT[gi * 64:(gi + 1) * 64, tok], ao.t[:], [ao], [self.db('MIXT', (gi * 64, t))])
            aprev = acur

    Core.even_setup = even_setup
    Core.even_scratch = even_scratch
    Core.phaseA_even = phaseA_even


_even_methods()


def _attn_methods():
    def attn_scratch(self):
        T, NT = self.T, self.NT
        QC = self.QC = min(512, T)
        self.ktr = self.ring("ktr", [128, 2, T], BF16, 1)
        self.vhr = self.ring("vhr", [128, NT, 128], BF16, 1)
        self.qtr = self.ring("qtr", [128, 2, QC], BF16, 2)
        self.gtr = self.ring("gtr", [128, QC], BF16, 2)
        self.ptr = self.ring("ptr", [128, QC], BF16, 4)
        self.accr = self.ring("accr", [128, QC], F32, 4)
        self.for_ = self.ring("fo", [128, QC], F32, 4)
        self.rlr = self.ring("rl", [128, QC], F32, 2)
        self.fw = [self.sb("fw%d" % i, [128, QC], F32) for i in range(3)]
        self.sqb = self.sb("sqb", [128, QC], BF16)
        self.obr = self.ring("obr", [128, QC], BF16, 2)

    def load_v(self, vh, h):
        NT = self.NT
        step = 16
        for n0 in range(0, NT, step):
            n1 = min(NT, n0 + step)
            self.dma('sp', vh.t[:, n0:n1, :], self.V[n0 * 128:n1 * 128, h * 128:(h + 1) * 128].rearrange("(n p) e -> p n e", p=128),
                     [self.db('V', (h, t)) for t in range(n0, n1)], [vh])

    def attention(self, h, kt, vh, mla, scale, load_q, finalize):
        A = self.A
        T, QC = self.T, self.QC
        nd = QC // 128
        maps = [0] if mla else [0, 1]
        tiles = []
        gi = 0
        for qc in range(T // QC):
            for m in maps:
                nkb = (qc + 1) * nd
                for kb in range(nkb):
                    tiles.append((gi, qc, m, kb, kb == 0, kb == nkb - 1, kb - qc * nd))
                gi += 1
        qts = {}
        state = {}

        def get_q(qc):
            if qc not in qts:
                qts[qc] = load_q(qc)
            return qts[qc]

        def issue_qk(i):
            gi, qc, m, kb, first, last, jd = tiles[i]
            qt, gt = get_q(qc)
            psS = self.PS[i % 2]
            c0 = 128 * jd if jd > 0 else 0
            ks = slice(kb * 128, (kb + 1) * 128)
            diag = jd >= 0
            if mla:
                A('pe', 'matmul', [kt, qt], [psS], psS.t[:, c0:QC], lhsT=kt.t[:, 0, ks], rhs=qt.t[:, 0, c0:QC], start=True, stop=False)
                A('pe', 'matmul', [kt, qt], [psS], psS.t[:, c0:QC], lhsT=kt.t[0:64, 1, ks], rhs=qt.t[0:64, 1, c0:QC], start=False, stop=not diag)
            else:
                r = slice(64 * m, 64 * m + 64)
                A('pe', 'matmul', [kt, qt], [psS], psS.t[:, c0:QC], lhsT=kt.t[r, 0, ks], rhs=qt.t[r, 0, c0:QC], start=True, stop=not diag)
            if diag:
                A('pe', 'matmul', [self.identb, self.cmask], [psS], psS.t[:, c0:QC], lhsT=self.identb.t[:], rhs=self.cmask.t[:, jd, c0:QC], start=False, stop=True)

        issue_qk(0)
        for i, (gi, qc, m, kb, first, last, jd) in enumerate(tiles):
            if i + 1 < len(tiles):
                issue_qk(i + 1)
            psS = self.PS[i % 2]
            psO = self.PS[2 + gi % 4]
            c0 = 128 * jd if jd > 0 else 0
            if first:
                state[gi] = self.accr.next()
            acc = state[gi]
            pt = self.ptr.next()
            A('act', 'activation', [psS], [pt], out=pt.t[:, c0:QC], in_=psS.t[:, c0:QC], func=AF.Exp, scale=scale)
            A('pe', 'matmul', [vh, pt], [psO], psO.t[:, c0:QC], lhsT=vh.t[:, kb, :], rhs=pt.t[:, c0:QC], start=first, stop=last)
            if first:
                A('dve', 'tensor_copy', [pt], [acc], out=acc.t[:], in_=pt.t[:])
            else:
                A('dve', 'tensor_tensor', [pt, acc], [acc], out=acc.t[:, c0:QC], in0=acc.t[:, c0:QC], in1=pt.t[:, c0:QC], op=ALU.add)
            if last:
                fo = self.for_.next()
                A('act', 'copy', [psO], [fo], out=fo.t[:], in_=psO.t[:, 0:QC])
                psL = self.PS[6]
                A('pe', 'matmul', [self.onesf, acc], [psL], psL.t[:, 0:QC], lhsT=self.onesf.t[:], rhs=acc.t[:], start=True, stop=True)
                rl = self.rlr.next()
                A('dve', 'reciprocal', [psL], [rl], out=rl.t[:], in_=psL.t[:, 0:QC])
                A('pool', 'tensor_tensor', [fo, rl], [fo], out=fo.t[:], in0=fo.t[:], in1=rl.t[:], op=ALU.mult)
                state[('o', qc, m)] = fo
                if m == maps[-1]:
                    qt, gt = qts.pop(qc)
                    finalize(qc, [state.pop(('o', qc, mm)) for mm in maps], gt)

    def phaseB_even(self, li):
        T, NT = self.T, self.NT
        A, dma = self.A, self.dma
        self.phase_reset()
        self.attn_scratch()
        QC = self.QC
        nd = QC // 128
        l2 = self.lam2
        for h in range(6):
            kt = self.ktr.next()
            vh = self.vhr.next()
            dma('sp', kt.t[:, 0, :], self.KT[h], [self.db('KT', (h, t)) for t in range(NT)], [kt])
            self.load_v(vh, h)

            def load_q(qc, h=h):
                qs = slice(qc * QC, (qc + 1) * QC)
                qt = self.qtr.next()
                gt = self.gtr.next()
                dma('sp', qt.t[:, 0, :], self.QT[h, :, qs], [self.db('QT', (h, qc * nd + i)) for i in range(nd)], [qt])
                dma('sp', gt.t[:], self.GT[h * 128:(h + 1) * 128, qs], [self.db('GT', (h, qc * nd + i)) for i in range(nd)], [gt])
                return qt, gt

            def finalize(qc, os_, gt, h=h):
                o = self.fw[0]
                A('dve', 'scalar_tensor_tensor', [os_[0], os_[1], l2], [o], out=o.t[:], in0=os_[1].t[:], scalar=l2.t[:, 5:6], in1=os_[0].t[:],
                  op0=ALU.mult, op1=ALU.add)
                self.finish_head(o, gt, l2.t[:, 6:7], [l2], 256 + h * 128, qc)

            self.attention(h, kt, vh, False, 0.125, load_q, finalize)

    def finish_head(self, o, gt, gain_col, gain_deps, row0, qc):
        A, dma = self.A, self.dma
        QC = self.QC
        nd = QC // 128
        qs = slice(qc * QC, (qc + 1) * QC)
        f = self.fw
        ps = self.PS[7]
        A('act', 'activation', [o], [self.sqb], out=self.sqb.t[:], in_=o.t[:], func=AF.Square)
        A('pe', 'matmul', [self.onesb, self.sqb], [ps], ps.t[:, 0:QC], lhsT=self.onesb.t[:], rhs=self.sqb.t[:], start=True, stop=True)
        self.rstd((ps, ps.t[:, 0:QC]), 128, (f[1], f[1].t[:]), (f[1], f[1].t[:]))
        A('pool', 'tensor_tensor', [o, f[1]], [f[2]], out=f[2].t[:], in0=o.t[:], in1=f[1].t[:], op=ALU.mult)
        ob = self.obr.next()
        A('dve', 'scalar_tensor_tensor', [f[2], gt] + gain_deps, [ob], out=ob.t[:], in0=f[2].t[:], scalar=gain_col, in1=gt.t[:],
          op0=ALU.mult, op1=ALU.mult)
        dma('sp', self.MIXT[row0:row0 + 128, qs], ob.t[:], [ob], [self.db('MIXT', (row0, qc * nd + i)) for i in range(nd)])

    Core.attn_scratch = attn_scratch
    Core.attention = attention
    Core.load_v = load_v
    Core.phaseB_even = phaseB_even
    Core.finish_head = finish_head


_attn_methods()


def _phaseC():
    def c_setup(self, li):
        s = "l%d_" % li
        W = {}
        W['wout'] = self.din(s + "wout", [D, D])
        W['wg'] = self.din(s + "wg", [D, D])
        W['wp'] = self.din(s + "wp", [256, D])
        return W

    def c_scratch(self):
        self.woutb = self.sb("woutb", [128, 8, D], BF16)
        self.wgb = self.sb("wgb", [128, 8, D], BF16)
        self.wpb = self.sb("wpb", [128, 2, D], BF16)
        self.mxT = self.ring("mxT", [128, 8, 128], BF16, 2)
        self.h1 = self.ring("h1", [128, D], F32, 2)
        self.h1b = self.ring("h1b", [128, D], BF16, 2)
        self.pf = self.ring("pf", [128, 256], F32, 2)
        self.pbf = self.ring("pbf", [128, 256], BF16, 2)
        self.pT = self.ring("pT", [128, 2, 128], BF16, 2)
        self.sg = self.ring("sg", [128, D], F32, 2)

    def phaseC(self, li, W, Hin, Hout, mix_rows):
        T, NT = self.T, self.NT
        A, dma = self.A, self.dma
        self.phase_reset()
        self.c_scratch()
        dma('pool', self.woutb.t[:], W['wout'].rearrange("(k p) n -> p k n", p=128), [], [self.woutb])
        dma('pool', self.wgb.t[:], W['wg'].rearrange("(k p) n -> p k n", p=128), [], [self.wgb])
        dma('pool', self.wpb.t[:], W['wp'].rearrange("(k p) n -> p k n", p=128), [], [self.wpb])
        for t in range(NT):
            tok = slice(t * 128, (t + 1) * 128)
            hs = self.hs.next()
            dma('sp', hs.t[:], Hin[tok, :], [self.db('H', t)], [hs])
            mx = self.mxT.next()
            dma('sp', mx.t[:], self.MIXT[:, tok].rearrange("(k p) t -> p k t", p=128), [self.db('MIXT', (r, t)) for r in mix_rows], [mx])
            pf = self.pf.next()
            dma('sp', pf.t[:], self.p[li, tok, :], [], [pf])
            h1 = self.h1.next()
            for c in range(2):
                ps = self.psn()
                for k in range(8):
                    A('pe', 'matmul', [mx, self.woutb], [ps], ps.t[:], lhsT=mx.t[:, k, :], rhs=self.woutb.t[:, k, c * 512:(c + 1) * 512],
                      start=(k == 0), stop=(k == 7))
                A('dve', 'tensor_tensor', [ps, hs], [h1], out=h1.t[:, c * 512:(c + 1) * 512], in0=ps.t[:], in1=hs.t[:, c * 512:(c + 1) * 512], op=ALU.add)
            h1b = self.h1b.next()
            A('act', 'copy', [h1], [h1b], out=h1b.t[:], in_=h1.t[:])
            hT = self.transpose_scale(h1b, None, 8)
            pb = self.pbf.next()
            A('pool', 'tensor_copy', [pf], [pb], out=pb.t[:], in_=pf.t[:])
            pT = self.transpose_scale(pb, None, 2, out=self.pT.next())
            sg = self.sg.next()
            for c in range(2):
                cs_ = slice(c * 512, (c + 1) * 512)
                ps = self.psn()
                for k in range(8):
                    A('pe', 'matmul', [hT, self.wgb], [ps], ps.t[:], lhsT=hT.t[:, k, :], rhs=self.wgb.t[:, k, cs_], start=(k == 0), stop=(k == 7))
                A('act', 'activation', [ps], [sg], out=sg.t[:, cs_], in_=ps.t[:], func=AF.Sigmoid)
                ps2 = self.psn()
                for k in range(2):
                    A('pe', 'matmul', [pT, self.wpb], [ps2], ps2.t[:], lhsT=pT.t[:, k, :], rhs=self.wpb.t[:, k, cs_], start=(k == 0), stop=(k == 1))
                A('dve', 'tensor_tensor', [ps2, sg], [sg], out=sg.t[:, cs_], in0=ps2.t[:], in1=sg.t[:, cs_], op=ALU.mult)
                A('pool', 'tensor_tensor', [sg, h1], [h1], out=h1.t[:, cs_], in0=sg.t[:, cs_], in1=h1.t[:, cs_], op=ALU.add)
            dma('sp', Hout[tok, :], h1.t[:], [h1], [self.db('H', t)])

    Core.c_setup = c_setup
    Core.c_scratch = c_scratch
    Core.phaseC = phaseC


_phaseC()


def build(T, layers=(0, 1, 2, 3), dbg=()):
    c = Core(T, layers, dbg)
    c.setup()
    Ws = {}
    for li in layers:
        Ws[li] = (c.even_setup(li // 2) if li % 2 == 0 else c.odd_setup(li // 2), c.c_setup(li))
    for n, li in enumerate(layers):
        Hin = c.x if n == 0 else c.H
        Hout = c.y if n == len(layers) - 1 else c.H
        Wa, Wc = Ws[li]
        if li % 2 == 0:
            c.phaseA_even(li, li // 2, Wa, Hin)
            c.phaseB_even(li)
            rows = [0, 64, 128, 192] + [256 + h * 128 for h in range(6)]
        else:
            c.phaseA_odd(li, li // 2, Wa, Hin)
            c.phaseB_odd(li)
            rows = [h * 128 for h in range(8)]
        c.phaseC(li, Wc, Hin, Hout, rows)
    c.P.emit()
    print("SBUF arena max bytes/partition:", c.maxoff, "ops:", {e: len(st) for e, st in c.P.streams.items()})
    return c.nc


def prep_core_inputs(inp, b, T, layers=(0, 1, 2, 3)):
    f = lambda a: np.ascontiguousarray(a, dtype=np.float32)
    m = {}
    m['x'] = f(inp['x'][b, :T])
    m['p'] = f(inp['p'][:, b, :T])
    m['pos'] = np.ascontiguousarray(inp['positions'][b, :T].reshape(T // 128, 128).T.astype(np.int32))
    for k, v in host_consts().items():
        m['c_' + k] = v
    for li in layers:
        j = li // 2
        s = "l%d_" % li
        m[s + 'gn'] = f(inp['norm_g'][li].reshape(8, 128).T)
        m[s + 'wg'] = f(inp['ple_w_gate'][li])
        m[s + 'wp'] = f(inp['ple_w_proj'][li])
        if li % 2 == 0:
            e = "e%d_" % j
            w = inp['ev_w_in'][j]
            a_in, a_gate, q, k, v, b_gate = np.split(w, np.cumsum([256, 256, 768, 768, 768, 768])[:-1], axis=1)
            m[e + 'wtm'] = f(np.concatenate([q, k, v, a_in], axis=1))
            m[e + 'wfm'] = f(np.concatenate([a_gate, b_gate], axis=1))
            m[e + 'poolw'] = f(inp['ev_pool_w'][j].transpose(1, 0, 2))
            m[e + 'pscale'] = f(inp['ev_pool_scale'][j].reshape(4, 64).T)
            m[e + 'qg'] = f(np.broadcast_to(inp['ev_q_norm'][j][None, :], (128, 64)))
            m[e + 'kg'] = f(np.broadcast_to(inp['ev_k_norm'][j][None, :], (128, 64)))
            m[e + 'lam'] = f(np.broadcast_to(inp['ev_lambda'][j].reshape(1, 256), (128, 256)))
            m[e + 'subln'] = f(inp['ev_subln'][j].reshape(128, 1))
            m[s + 'wout'] = f(inp['ev_w_out'][j])
        else:
            m.update(prep_odd(inp, j))
            m[s + 'wout'] = f(inp['od_w_out'][j])
    return m


def prep_odd(inp, j):
    f = lambda a: np.ascontiguousarray(a, dtype=np.float32)
    o = "o%d_" % j
    m = {}
    w = inp['od_w_in'][j]
    cq, ckv, krope, cgate, conv, d_b, d_a, dgate = np.split(w, np.cumsum([256, 128, 64, 512, 1536, 4, 4, 512])[:-1], axis=1)
    m[o + 'wtm'] = f(np.concatenate([cq, ckv, krope], axis=1))
    m[o + 'wfm'] = f(np.concatenate([cgate, conv], axis=1))
    m[o + 'wch'] = f(np.concatenate([dgate, d_b, d_a], axis=1))
    m[o + 'qag'] = f(inp['od_q_a_norm'][j].reshape(2, 128).T)
    m[o + 'kvag'] = f(inp['od_kv_a_norm'][j].reshape(1, 128).T)
    m[o + 'wuq'] = f(inp['od_w_uq'][j])
    m[o + 'wukv'] = f(inp['od_w_ukv'][j])
    m[o + 'qg'] = f(np.broadcast_to(inp['od_q_norm'][j][None, :], (128, 192)))
    m[o + 'kg'] = f(np.broadcast_to(inp['od_k_norm'][j][None, :], (128, 192)))
    m[o + 'convw'] = f(inp['od_conv_w'][j].reshape(4, 12, 128).transpose(2, 1, 0))
    m[o + 'alog'] = f(np.broadcast_to(inp['od_a_log'][j][None, :], (64, 4)))
    m[o + 'dtb'] = f(np.broadcast_to(inp['od_dt_bias'][j][None, :], (64, 4)))
    m[o + 'onorm'] = f(np.broadcast_to(inp['od_o_norm'][j][None, :], (64, 128)))
    return m


def _odd_methods():
    def odd_setup(self, j):
        s = "o%d_" % j
        W = {}
        for nm, shp in (('wtm', [D, 448]), ('wfm', [D, 2048]), ('wch', [D, 520]), ('qag', [128, 2]), ('kvag', [128, 1]),
                        ('wuq', [256, 768]), ('wukv', [128, 1024]), ('qg', [128, 192]), ('kg', [128, 192]),
                        ('convw', [128, 12, 4]), ('alog', [64, 4]), ('dtb', [64, 4]), ('onorm', [64, 128])):
            W[nm] = self.din(s + nm, shp)
        return W

    def odd_scratch(self):
        T = self.T
        sb, ring = self.sb, self.ring
        if not hasattr(self, 'QTM'):
            self.QTM = self.dscr("QTM", [4, 2, 128, T], BF16)
            self.KTM = self.dscr("KTM", [4, 2, 128, T], BF16)
        if not hasattr(self, 'V'):
            self.V = self.dscr("V", [T, 768], BF16)
            self.GT = self.dscr("GT", [768, T], BF16)
        self.owtm = sb("owtm", [128, 8, 448], BF16)
        self.owfm = sb("owfm", [128, 8, 2048], BF16)
        self.owch = sb("owch", [128, 8, 520], BF16)
        self.qag = sb("qag", [128, 2])
        self.kvag = sb("kvag", [128, 1])
        self.wuq = sb("wuq", [128, 2, 768], BF16)
        self.wukv = sb("wukv", [128, 1024], BF16)
        self.oqg = sb("oqg", [128, 192])
        self.okg = sb("okg", [128, 192])
        self.convw = sb("convw", [128, 12, 4])
        self.alog = sb("alog", [64, 4])
        self.dtb = sb("dtb", [64, 4])
        self.onorm = sb("onorm", [64, 128])
        self.nexpa = sb("nexpa", [64, 4])
        self.cqb = ring("cqb", [128, 384], BF16, 2)
        self.cT = ring("cT", [128, 3, 128], BF16, 2)
        self.qk3 = ring("qk3", [128, 4, 192], F32, 2)
        self.qk3b = ring("qk3b", [128, 4, 192], BF16, 2)
        self.rt32 = ring("rt32", [128, 4, 32], F32, 8)
        self.qT8 = ring("qT8", [128, 8, 128], BF16, 2)
        self.vb4 = ring("vb4", [128, 4, 128], BF16, 2)
        self.xc = [sb("xc%d" % i, [128, 12, 131]) for i in range(2)]
        self.tmb = ring("tmb", [128, 512], BF16, 2)
        self.tmf = ring("tmf", [128, 512], F32, 2)
        self.gt4 = ring("gt4", [128, 4, 128], BF16, 2)
        self.cvA = sb("cvA", [128, 12, 128])
        self.cvB = sb("cvB", [128, 12, 128])
        self.qkv = ring("qkv", [128, 12, 128], F32, 2)
        self.S = [[sb("S%d_%d" % (h, i), [128, 128]) for i in range(2)] for h in range(4)]
        self.g64 = ring("g64", [64, 24], F32, 4)
        self.dg = ring("dg", [64, 512], F32, 2)
        self.c64 = ring("c64", [64, 8], F32, 8)
        self.ob64 = ring("ob64", [64, 128], BF16, 4)
        self.mixt = ring("mixt", [128, 4, 128], BF16, 2)

    def mla_norm_rope(self, src, gain, t, dst, dname, rp):
        A, dma = self.A, self.dma
        tok = slice(t * 128, (t + 1) * 128)
        sm = self.small.next()
        A('act', 'activation', [src], [self.junk], out=self.junk.t[:, 0:768], in_=src.t[:, :, :].rearrange("p a b -> p (a b)"), func=AF.Square)
        A('dve', 'reduce_sum', [self.junk], [sm], out=sm.t[:, 0:4], in_=self.junk.t[:, 0:768].rearrange("p (a b) -> p a b", b=192), axis=AX.X)
        self.rstd((sm, sm.t[:, 0:4]), 192, (sm, sm.t[:, 4:8]), (sm, sm.t[:, 4:8]))
        A('dve', 'tensor_tensor', [src, sm], [src], out=src.t[:], in0=src.t[:], in1=sm.t[:, 4:8].unsqueeze(2).broadcast_to([128, 4, 192]), op=ALU.mult)
        A('pool', 'tensor_tensor', [src, gain], [src], out=src.t[:], in0=src.t[:], in1=gain.t[:, :].unsqueeze(1).broadcast_to([128, 4, 192]), op=ALU.mult)
        qb = self.qk3b.next()
        A('act', 'copy', [src], [qb], out=qb.t[:], in_=src.t[:])
        cs = rp.t[:, 8:40].unsqueeze(1).broadcast_to([128, 4, 32])
        sn = rp.t[:, 48:80].unsqueeze(1).broadcast_to([128, 4, 32])
        t1, t2, t3, t4 = self.rt32.next(), self.rt32.next(), self.rt32.next(), self.rt32.next()
        A('dve', 'tensor_tensor', [src, rp], [t1], out=t1.t[:], in0=src.t[:, :, 128:160], in1=cs, op=ALU.mult)
        A('pool', 'tensor_tensor', [src, rp], [t2], out=t2.t[:], in0=src.t[:, :, 160:192], in1=sn, op=ALU.mult)
        A('dve', 'tensor_tensor', [src, rp], [t3], out=t3.t[:], in0=src.t[:, :, 160:192], in1=cs, op=ALU.mult)
        A('pool', 'tensor_tensor', [src, rp], [t4], out=t4.t[:], in0=src.t[:, :, 128:160], in1=sn, op=ALU.mult)
        A('dve', 'tensor_tensor', [t1, t2, qb], [qb], out=qb.t[:, :, 128:160], in0=t1.t[:], in1=t2.t[:], op=ALU.subtract)
        A('dve', 'tensor_tensor', [t3, t4, qb], [qb], out=qb.t[:, :, 160:192], in0=t3.t[:], in1=t4.t[:], op=ALU.add)
        ps = self.psn()
        pv = ps.t[:, :].bitcast(BF16)
        for h in range(4):
            A('pe', 'transpose', [qb, self.identb], [ps], out=pv[:, h * 128:(h + 1) * 128], in_=qb.t[:, h, 0:128], identity=self.identb.t[:])
            A('pe', 'transpose', [qb, self.identb], [ps], out=pv[0:64, 512 + h * 128:512 + (h + 1) * 128], in_=qb.t[:, h, 128:192], identity=self.identb.t[:])
        qT = self.qT8.next()
        A('act', 'copy', [ps], [qT], out=qT.t[:, 0:4, :], in_=pv[:, 0:512].rearrange("p (a b) -> p a b", b=128))
        A('dve', 'tensor_copy', [ps], [qT], out=qT.t[0:64, 4:8, :], in_=pv[0:64, 512:1024].rearrange("p (a b) -> p a b", b=128))
        dma('sp', dst[:, 0, :, tok].rearrange("h p t -> p h t"), qT.t[:, 0:4, :], [qT], [self.db(dname, (h, t)) for h in range(4)])
        dma('sp', dst[:, 1, 0:64, tok].rearrange("h p t -> p h t"), qT.t[0:64, 4:8, :], [qT], [self.db(dname, (h, t)) for h in range(4)])

    Core.odd_setup = odd_setup
    Core.odd_scratch = odd_scratch
    Core.mla_norm_rope = mla_norm_rope


_odd_methods()


def _odd_phaseA():
    def phaseA_odd(self, li, j, W, Hin):
        T, NT = self.T, self.NT
        A, dma = self.A, self.dma
        self.phase_reset()
        self.odd_scratch()
        self.gdn_scratch()
        g = self.gnorm[li]
        dma('pool', self.owtm.t[:], W['wtm'].rearrange("(k p) n -> p k n", p=128), [], [self.owtm])
        dma('pool', self.owfm.t[:], W['wfm'].rearrange("(k p) n -> p k n", p=128), [], [self.owfm])
        dma('pool', self.owch.t[:], W['wch'].rearrange("(k p) n -> p k n", p=128), [], [self.owch])
        dma('pool', self.wuq.t[:], W['wuq'].rearrange("(k p) n -> p k n", p=128), [], [self.wuq])
        dma('pool', self.wukv.t[:], W['wukv'], [], [self.wukv])
        for nm, t in (('qag', self.qag), ('kvag', self.kvag), ('qg', self.oqg), ('kg', self.okg), ('convw', self.convw),
                      ('alog', self.alog), ('dtb', self.dtb), ('onorm', self.onorm)):
            dma('sp', t.t[:], W[nm], [], [t])
        A('act', 'activation', [self.alog], [self.nexpa], out=self.nexpa.t[:], in_=self.alog.t[:], func=AF.Exp)
        A('dve', 'tensor_scalar', [self.nexpa], [self.nexpa], out=self.nexpa.t[:], in0=self.nexpa.t[:], scalar1=-1.0, scalar2=None, op0=ALU.mult)
        for h in range(4):
            A('dve', 'memset', [], [self.S[h][0]], self.S[h][0].t[:], 0.0)
        A('dve', 'memset', [], [self.xc[1]], self.xc[1].t[:, :, 128:131], 0.0)
        for t in range(NT):
            tok = slice(t * 128, (t + 1) * 128)
            hs = self.hs.next()
            dma('sp', hs.t[:], Hin[tok, :], [self.db('H', t)], [hs])
            xT = self.norm_transpose(hs, g)
            rp = self.rope_tile(t)
            ps1 = self.psn()
            for k in range(8):
                A('pe', 'matmul', [xT, self.owtm], [ps1], ps1.t[:, 0:448], lhsT=xT.t[:, k, :], rhs=self.owtm.t[:, k, :], start=(k == 0), stop=(k == 7))
            sm = self.small.next()
            A('act', 'activation', [ps1], [self.junk, sm], out=self.junk.t[:, 0:256], in_=ps1.t[:, 0:256], func=AF.Square, accum_out=sm.t[:, 0:1])
            A('act', 'activation', [ps1], [self.junk, sm], out=self.junk.t[:, 256:384], in_=ps1.t[:, 256:384], func=AF.Square, accum_out=sm.t[:, 1:2])
            self.rstd((sm, sm.t[:, 0:1]), 256, (sm, sm.t[:, 2:3]), (sm, sm.t[:, 2:3]))
            self.rstd((sm, sm.t[:, 1:2]), 128, (sm, sm.t[:, 3:4]), (sm, sm.t[:, 3:4]))
            cqb = self.cqb.next()
            A('dve', 'tensor_scalar', [ps1, sm], [cqb], out=cqb.t[:, 0:256], in0=ps1.t[:, 0:256], scalar1=sm.t[:, 2:3], scalar2=None, op0=ALU.mult)
            A('dve', 'tensor_scalar', [ps1, sm], [cqb], out=cqb.t[:, 256:384], in0=ps1.t[:, 256:384], scalar1=sm.t[:, 3:4], scalar2=None, op0=ALU.mult)
            ps = self.psn()
            pv = ps.t[:, :].bitcast(BF16)
            for k in range(3):
                A('pe', 'transpose', [cqb, self.identb], [ps], out=pv[:, k * 128:(k + 1) * 128], in_=cqb.t[:, k * 128:(k + 1) * 128], identity=self.identb.t[:])
            cT = self.cT.next()
            for k in range(3):
                gcol = self.qag.t[:, k:k + 1] if k < 2 else self.kvag.t[:, 0:1]
                A('dve', 'tensor_scalar', [ps, self.qag, self.kvag], [cT], out=cT.t[:, k, :], in0=pv[:, k * 128:(k + 1) * 128], scalar1=gcol, scalar2=None, op0=ALU.mult)
            q3 = self.qk3.next()
            for c in range(2):
                psq = self.psn()
                for k in range(2):
                    A('pe', 'matmul', [cT, self.wuq], [psq], psq.t[:, 0:384], lhsT=cT.t[:, k, :], rhs=self.wuq.t[:, k, c * 384:(c + 1) * 384], start=(k == 0), stop=(k == 1))
                A('act', 'copy', [psq], [q3], out=q3.t[:, 2 * c:2 * c + 2, :], in_=psq.t[:, 0:384].rearrange("p (a b) -> p a b", b=192))
            self.mla_norm_rope(q3, self.oqg, t, self.QTM, 'QTM', rp)
            k3 = self.qk3.next()
            vb = self.vb4.next()
            for c in range(2):
                psk = self.psn()
                A('pe', 'matmul', [cT, self.wukv], [psk], psk.t[:], lhsT=cT.t[:, 2, :], rhs=self.wukv.t[:, c * 512:(c + 1) * 512], start=True, stop=True)
                kv4 = psk.t[:, :].rearrange("p (a b) -> p a b", b=256)
                A('act', 'copy', [psk], [k3], out=k3.t[:, 2 * c:2 * c + 2, 0:128], in_=kv4[:, :, 0:128])
                A('dve', 'tensor_copy', [psk], [vb], out=vb.t[:, 2 * c:2 * c + 2, :], in_=kv4[:, :, 128:256])
            A('dve', 'tensor_copy', [ps1], [k3], out=k3.t[:, :, 128:192], in_=ps1.t[:, 384:448].unsqueeze(1).broadcast_to([128, 4, 64]))
            dma('sp', self.V[tok, 0:512], vb.t[:, :, :].rearrange("p a b -> p (a b)"), [vb], [self.db('V', (h, t)) for h in range(4)])
            self.mla_norm_rope(k3, self.okg, t, self.KTM, 'KTM', rp)
            xc = self.xc[t % 2]
            xp = self.xc[(t + 1) % 2]
            A('pool', 'tensor_copy', [xp], [xc], out=xc.t[:, :, 0:3], in_=xp.t[:, :, 128:131])
            for gi in range(4):
                ps = self.psn()
                for k in range(8):
                    A('pe', 'matmul', [xT, self.owfm], [ps], ps.t[:], lhsT=xT.t[:, k, :], rhs=self.owfm.t[:, k, gi * 512:(gi + 1) * 512],
                      start=(k == 0), stop=(k == 7))
                if gi == 0:
                    tb = self.tmb.next()
                    A('act', 'activation', [ps], [tb], out=tb.t[:], in_=ps.t[:], func=AF.Silu)
                    pst = self.psn()
                    pv = pst.t[:, :].bitcast(BF16)
                    for hh in range(4):
                        A('pe', 'transpose', [tb, self.identb], [pst], out=pv[:, hh * 128:(hh + 1) * 128], in_=tb.t[:, hh * 128:(hh + 1) * 128],
                          identity=self.identb.t[:])
                    g4 = self.gt4.next()
                    A('dve', 'tensor_copy', [pst], [g4], out=g4.t[:], in_=pv[:, 0:512].rearrange("p (a b) -> p a b", b=128))
                    dma('sp', self.GT[0:512, tok].rearrange("(h p) t -> p h t", p=128), g4.t[:], [g4], [self.db('GT', (hh, t)) for hh in range(4)])
                else:
                    tf = self.tmf.next()
                    if gi % 2:
                        A('act', 'copy', [ps], [tf], out=tf.t[:], in_=ps.t[:])
                    else:
                        A('dve', 'tensor_copy', [ps], [tf], out=tf.t[:], in_=ps.t[:])
                    pst = self.psn()
                    for jj in range(4):
                        A('pe', 'matmul', [tf, self.identf], [pst], pst.t[:, jj * 128:(jj + 1) * 128], lhsT=tf.t[:, jj * 128:(jj + 1) * 128],
                          rhs=self.identf.t[:], start=True, stop=True)
                    c0 = (gi - 1) * 4
                    A('act', 'copy', [pst], [xc], out=xc.t[:, c0:c0 + 4, 3:131], in_=pst.t[:, :].rearrange("p (a b) -> p a b", b=128))
            acc = self.cvA
            tm = self.cvB
            wb = lambda jj: self.convw.t[:, :, jj:jj + 1].broadcast_to([128, 12, 128])
            A('dve', 'tensor_tensor', [xc, self.convw], [acc], out=acc.t[:], in0=xc.t[:, :, 0:128], in1=wb(0), op=ALU.mult)
            A('pool', 'tensor_tensor', [xc, self.convw], [tm], out=tm.t[:], in0=xc.t[:, :, 1:129], in1=wb(1), op=ALU.mult)
            A('dve', 'tensor_tensor', [acc, tm], [acc], out=acc.t[:], in0=acc.t[:], in1=tm.t[:], op=ALU.add)
            A('pool', 'tensor_tensor', [xc, self.convw], [tm], out=tm.t[:], in0=xc.t[:, :, 2:130], in1=wb(2), op=ALU.mult)
            A('dve', 'tensor_tensor', [acc, tm], [acc], out=acc.t[:], in0=acc.t[:], in1=tm.t[:], op=ALU.add)
            A('pool', 'tensor_tensor', [xc, self.convw], [tm], out=tm.t[:], in0=xc.t[:, :, 3:131], in1=wb(3), op=ALU.mult)
            A('dve', 'tensor_tensor', [acc, tm], [acc], out=acc.t[:], in0=acc.t[:], in1=tm.t[:], op=ALU.add)
            qkv = self.qkv.next()
            A('act', 'activation', [acc], [qkv], out=qkv.t[:], in_=acc.t[:], func=AF.Silu)
            sq8, r8 = self.cvB, self.cvA
            A('act', 'activation', [qkv], [sq8], out=sq8.t[:, 0:8, :], in_=qkv.t[:, 0:8, :], func=AF.Square)
            for c in range(2):
                ps = self.psn()
                A('pe', 'matmul', [self.onesf, sq8], [ps], ps.t[:], lhsT=self.onesf.t[:], rhs=sq8.t[:, 4 * c:4 * c + 4, :].rearrange("p a b -> p (a b)"), start=True, stop=True)
                r8v = r8.t[:, 4 * c:4 * c + 4, :].rearrange("p a b -> p (a b)")
                self.rstd((ps, ps.t[:]), 1.0, (r8, r8v), (r8, r8v))
            A('dve', 'tensor_scalar', [r8], [r8], out=r8.t[:, 0:4, :], in0=r8.t[:, 0:4, :], scalar1=128.0 ** -0.5, scalar2=None, op0=ALU.mult)
            A('dve', 'tensor_tensor', [qkv, r8], [qkv], out=qkv.t[:, 0:8, :], in0=qkv.t[:, 0:8, :], in1=r8.t[:, 0:8, :], op=ALU.mult)
            mixt = self.mixt.next()
            for c in range(2):
                if 'gdn' not in SKIP:
                    self.gdn_chunk(t, c, xT, qkv, mixt)
            dma('sp', self.MIXT[512:1024, tok].rearrange("(h p) t -> p h t", p=128), mixt.t[:], [mixt], [self.db('MIXT', (512 + h * 128, t)) for h in range(4)])

    Core.phaseA_odd = phaseA_odd


_odd_phaseA()


def _gdn():
    def gdn_scratch(self):
        r = self.ring
        self.gr = {
            'gm': r("g_gm", [64, 128], F32, 4), 'GBs': r("g_GBs", [128, 64], F32, 4), 'EB': r("g_EB", [128, 64], F32, 4),
            'c8': r("g_c8", [64, 8], F32, 4), 'd': r("g_d", [64, 64], F32, 8), 'Gam': r("g_Gam", [64, 64], F32, 4),
            'GamT': r("g_GamT", [64, 64], F32, 4), 'A': r("g_A", [64, 64], F32, 4), 'AT': r("g_AT", [64, 64], F32, 4),
            'P': r("g_P", [64, 64], F32, 8), 'PT': r("g_PT", [64, 64], F32, 8), 'RT': r("g_RT", [64, 64], F32, 8),
            'ru': r("g_ru", [64, 128], F32, 4), 'rw': r("g_rw", [64, 128], F32, 4), 'kd': r("g_kd", [64, 128], F32, 4),
            'us': r("g_us", [64, 128], F32, 4), 'wT': r("g_wT", [128, 64], F32, 4), 'QKg': r("g_QKg", [64, 64], F32, 4),
            'qdT': r("g_qdT", [128, 64], F32, 4), 'vn': r("g_vn", [64, 128], F32, 4), 'on': r("g_on", [64, 128], F32, 4),
        }

    def gdn_chunk(self, t, c, xT, qkv, mixt):
        A = self.A
        R = self.gr
        cs = slice(c * 64, (c + 1) * 64)
        n = 2 * t + c
        I64 = self.identf.t[0:64, 0:64]
        G = self.gdnc
        Tri, mL, mLT, st01 = G.t[:, 0, :], G.t[:, 1, :], G.t[:, 2, :], G.t[:, 3, :]
        mm = lambda rd, ps, out, lhsT, rhs, start=True, stop=True: A('pe', 'matmul', rd, [ps], out, lhsT=lhsT, rhs=rhs, start=start, stop=stop)
        psg = self.psn()
        for k in range(8):
            mm([xT, self.owch], psg, psg.t[0:64, 0:512], xT.t[:, k, cs], self.owch.t[:, k, 0:512], k == 0, k == 7)
        psb = self.psn()
        for k in range(8):
            mm([xT, self.owch], psb, psb.t[0:64, 0:8], xT.t[:, k, cs], self.owch.t[:, k, 512:520], k == 0, k == 7)
        dg = self.dg.next()
        A('act', 'activation', [psg], [dg], out=dg.t[:], in_=psg.t[0:64, 0:512], func=AF.Silu)
        gg = self.g64.next()
        A('act', 'activation', [psb], [gg], out=gg.t[:, 0:4], in_=psb.t[0:64, 0:4], func=AF.Sigmoid)
        A('dve', 'tensor_tensor', [psb, self.dtb], [gg], out=gg.t[:, 4:8], in0=psb.t[0:64, 4:8], in1=self.dtb.t[:], op=ALU.add)
        A('act', 'activation', [gg], [gg], out=gg.t[:, 8:12], in_=gg.t[:, 4:8], func=AF.Exp)
        A('dve', 'tensor_scalar', [gg], [gg], out=gg.t[:, 8:12], in0=gg.t[:, 8:12], scalar1=1.0, scalar2=None, op0=ALU.add)
        A('act', 'activation', [gg], [gg], out=gg.t[:, 8:12], in_=gg.t[:, 8:12], func=AF.Ln)
        A('dve', 'tensor_tensor', [gg, self.nexpa], [gg], out=gg.t[:, 12:16], in0=gg.t[:, 8:12], in1=self.nexpa.t[:], op=ALU.mult)
        psc = self.psn()
        mm([G, gg], psc, psc.t[0:64, 0:4], Tri, gg.t[:, 12:16])
        A('dve', 'tensor_copy', [psc], [gg], out=gg.t[:, 16:20], in_=psc.t[0:64, 0:4])
        if GSTAGE <= 0:
            return
        H = range(4)
        qT = [qkv.t[:, h, cs] for h in H]
        kT = [qkv.t[:, 4 + h, cs] for h in H]
        vT = [qkv.t[:, 8 + h, cs] for h in H]
        bcol = [gg.t[:, h:h + 1] for h in H]
        gcol = [gg.t[:, 12 + h:13 + h] for h in H]
        GBs, EB, c8, Gam, GamT, Am, RT, P, PT = [{} for _ in range(9)]
        for h in H:
            gm = R['gm'].next()
            A('dve', 'tensor_scalar', [self.onesf, gg], [gm], out=gm.t[:], in0=self.onesf.t[0:64, :], scalar1=gcol[h], scalar2=None, op0=ALU.mult)
            ps = self.psn()
            mm([gm, G], ps, ps.t[:, 0:64], gm.t[:], Tri)
            GBs[h] = R['GBs'].next()
            EB[h] = R['EB'].next()
            c8[h] = R['c8'].next()
            A('act', 'copy', [ps], [GBs[h]], out=GBs[h].t[:], in_=ps.t[:, 0:64])
            A('act', 'activation', [ps], [EB[h]], out=EB[h].t[:], in_=ps.t[:, 0:64], func=AF.Exp)
            A('dve', 'tensor_copy', [gg], [c8[h]], out=c8[h].t[:, 0:1], in_=gg.t[:, 16 + h:17 + h])
            A('act', 'activation', [gg], [c8[h]], out=c8[h].t[:, 1:2], in_=gg.t[:, 16 + h:17 + h], func=AF.Exp)
            A('dve', 'tensor_tensor', [c8[h], gg], [c8[h]], out=c8[h].t[:, 2:3], in0=c8[h].t[:, 1:2], in1=bcol[h], op=ALU.mult)
            A('act', 'activation', [c8[h], GBs[h]], [c8[h]], out=c8[h].t[:, 3:4], in_=c8[h].t[:, 0:1], func=AF.Exp, scale=-1.0, bias=GBs[h].t[0:64, 63:64])
        if GSTAGE <= 1:
            return
        for h in H:
            d1, d2, e1, e2 = R['d'].next(), R['d'].next(), R['d'].next(), R['d'].next()
            Gam[h] = R['Gam'].next()
            GamT[h] = R['GamT'].next()
            A('dve', 'tensor_scalar', [GBs[h], c8[h]], [d1], out=d1.t[:], in0=GBs[h].t[0:64, :], scalar1=-1.0, scalar2=c8[h].t[:, 0:1], op0=ALU.mult, op1=ALU.add)
            A('dve', 'scalar_tensor_tensor', [d1, G], [d2], out=d2.t[:], in0=d1.t[:], scalar=0.0, in1=mL, op0=ALU.min, op1=ALU.add)
            A('act', 'activation', [d2], [Gam[h]], out=Gam[h].t[:], in_=d2.t[:], func=AF.Exp)
            A('dve', 'tensor_scalar', [GBs[h], c8[h]], [e1], out=e1.t[:], in0=GBs[h].t[0:64, :], scalar1=c8[h].t[:, 0:1], scalar2=None, op0=ALU.subtract)
            A('dve', 'scalar_tensor_tensor', [e1, G], [e2], out=e2.t[:], in0=e1.t[:], scalar=0.0, in1=mLT, op0=ALU.min, op1=ALU.add)
            A('act', 'activation', [e2], [GamT[h]], out=GamT[h].t[:], in_=e2.t[:], func=AF.Exp)
        if GSTAGE <= 2:
            return
        for h in H:
            ps = self.psn()
            mm([qkv], ps, ps.t[0:64, 0:64], kT[h], kT[h])
            if GSUB <= 0:
                continue
            a0 = R['d'].next()
            A('dve', 'scalar_tensor_tensor', [ps, gg, Gam[h]], [a0], out=a0.t[:], in0=ps.t[0:64, 0:64], scalar=bcol[h], in1=Gam[h].t[:], op0=ALU.mult, op1=ALU.mult)
            if GSUB <= 1:
                continue
            Am[h] = R['A'].next()
            A('pool', 'tensor_tensor', [a0, G], [Am[h]], out=Am[h].t[:], in0=a0.t[:], in1=st01, op=ALU.mult)
            if GSUB <= 2:
                continue
            ps2 = self.psn()
            mm([Am[h], self.identf], ps2, ps2.t[0:64, 0:64], Am[h].t[:], I64)
            if GSUB <= 3:
                continue
            at = R['AT'].next()
            A('act', 'copy', [ps2], [at], out=at.t[:], in_=ps2.t[0:64, 0:64])
            RT[h] = R['RT'].next()
            A('dve', 'scalar_tensor_tensor', [ps2, self.identf], [RT[h]], out=RT[h].t[:], in0=ps2.t[0:64, 0:64], scalar=-1.0, in1=I64, op0=ALU.mult, op1=ALU.add)
            P[h], PT[h] = Am[h], at
        if GSTAGE <= 3 or GSUB < 99:
            return
        for lvl in range(1, 6):
            for h in H:
                ps = self.psn()
                mm([PT[h], P[h]], ps, ps.t[0:64, 0:64], PT[h].t[:], P[h].t[:])
                pn = R['P'].next()
                A('act', 'copy', [ps], [pn], out=pn.t[:], in_=ps.t[0:64, 0:64])
                ptn = None
                if lvl < 5:
                    ps2 = self.psn()
                    mm([P[h], PT[h]], ps2, ps2.t[0:64, 0:64], P[h].t[:], PT[h].t[:])
                    ptn = R['PT'].next()
                    A('dve', 'tensor_copy', [ps2], [ptn], out=ptn.t[:], in_=ps2.t[0:64, 0:64])
                ps3 = self.psn()
                mm([pn, RT[h]], ps3, ps3.t[0:64, 0:64], pn.t[:], RT[h].t[:])
                rn = R['RT'].next()
                A('dve', 'tensor_tensor', [ps3, RT[h]], [rn], out=rn.t[:], in0=ps3.t[0:64, 0:64], in1=RT[h].t[:], op=ALU.add)
                P[h], PT[h], RT[h] = pn, ptn, rn
        if GSTAGE <= 4:
            return
        us, wTs, QKg, qdT, kdec = {}, {}, {}, {}, {}
        for h in H:
            psk = self.psn()
            mm([qkv, self.identf], psk, psk.t[0:64, 0:128], kT[h], self.identf.t[:])
            mm([qkv, self.identf], psk, psk.t[0:64, 128:256], vT[h], self.identf.t[:])
            ru, rw, kdec[h] = R['ru'].next(), R['rw'].next(), R['kd'].next()
            A('dve', 'tensor_scalar', [psk, gg], [ru], out=ru.t[:], in0=psk.t[0:64, 128:256], scalar1=bcol[h], scalar2=None, op0=ALU.mult)
            A('dve', 'tensor_scalar', [psk, c8[h]], [rw], out=rw.t[:], in0=psk.t[0:64, 0:128], scalar1=c8[h].t[:, 2:3], scalar2=None, op0=ALU.mult)
            A('act', 'activation', [psk, c8[h]], [kdec[h]], out=kdec[h].t[:], in_=psk.t[0:64, 0:128], func=AF.Copy, scale=c8[h].t[:, 3:4])
            psu = self.psn()
            mm([RT[h], ru], psu, psu.t[0:64, 0:128], RT[h].t[:], ru.t[:])
            us[h] = R['us'].next()
            A('act', 'copy', [psu], [us[h]], out=us[h].t[:], in_=psu.t[0:64, 0:128])
            psw = self.psn()
            mm([rw, RT[h]], psw, psw.t[:, 0:64], rw.t[:], RT[h].t[:])
            wTs[h] = R['wT'].next()
            A('dve', 'tensor_copy', [psw], [wTs[h]], out=wTs[h].t[:], in_=psw.t[:, 0:64])
            psq = self.psn()
            mm([qkv], psq, psq.t[0:64, 0:64], kT[h], qT[h])
            QKg[h] = R['QKg'].next()
            A('dve', 'tensor_tensor', [psq, GamT[h]], [QKg[h]], out=QKg[h].t[:], in0=psq.t[0:64, 0:64], in1=GamT[h].t[:], op=ALU.mult)
            qdT[h] = R['qdT'].next()
            A('pool', 'tensor_tensor', [qkv, EB[h]], [qdT[h]], out=qdT[h].t[:], in0=qT[h], in1=EB[h].t[:], op=ALU.mult)
        if GSTAGE <= 5:
            return
        vn, pso = {}, {}
        for h in H:
            S = self.S[h][n % 2]
            ps1 = self.psn()
            mm([wTs[h], S], ps1, ps1.t[0:64, 0:128], wTs[h].t[:], S.t[:])
            vn[h] = R['vn'].next()
            A('dve', 'scalar_tensor_tensor', [us[h], ps1], [vn[h]], out=vn[h].t[:], in0=ps1.t[0:64, 0:128], scalar=-1.0, in1=us[h].t[:], op0=ALU.mult, op1=ALU.add)
        if GSTAGE <= 6:
            return
        for h in H:
            S = self.S[h][n % 2]
            Sn = self.S[h][(n + 1) % 2]
            pso[h] = self.psn()
            mm([qdT[h], S], pso[h], pso[h].t[0:64, 0:128], qdT[h].t[:], S.t[:], True, False)
            mm([QKg[h], vn[h]], pso[h], pso[h].t[0:64, 0:128], QKg[h].t[:], vn[h].t[:], False, True)
            on = R['on'].next()
            sm = self.c64.next()
            A('act', 'activation', [pso[h]], [on, sm], out=on.t[:], in_=pso[h].t[0:64, 0:128], func=AF.Square, accum_out=sm.t[:, 0:1])
            self.rstd((sm, sm.t[:, 0:1]), 128, (sm, sm.t[:, 1:2]), (sm, sm.t[:, 2:3]))
            A('dve', 'scalar_tensor_tensor', [pso[h], sm, self.onorm], [on], out=on.t[:], in0=pso[h].t[0:64, 0:128], scalar=sm.t[:, 2:3], in1=self.onorm.t[:], op0=ALU.mult, op1=ALU.mult)
            ob = self.ob64.next()
            A('pool', 'tensor_tensor', [on, dg], [ob], out=ob.t[:], in0=on.t[:], in1=dg.t[:, h * 128:(h + 1) * 128], op=ALU.mult)
            pst = self.psn()
            pv = pst.t[:, :].bitcast(BF16)
            A('pe', 'transpose', [ob, self.identb], [pst], out=pv[:, 0:64], in_=ob.t[:], identity=self.identb.t[0:64, 0:64])
            A('act', 'copy', [pst], [mixt], out=mixt.t[:, h, cs], in_=pv[:, 0:64])
            ps3 = self.psn()
            mm([kdec[h], vn[h]], ps3, ps3.t[:, 0:128], kdec[h].t[:], vn[h].t[:])
            A('dve', 'scalar_tensor_tensor', [S, EB[h], ps3], [Sn], out=Sn.t[:], in0=S.t[:], scalar=EB[h].t[:, 63:64], in1=ps3.t[:, 0:128], op0=ALU.mult, op1=ALU.add)

    Core.gdn_scratch = gdn_scratch
    Core.gdn_chunk = gdn_chunk


_gdn()


def _phaseB_odd():
    def phaseB_odd(self, li):
        T, NT = self.T, self.NT
        A, dma = self.A, self.dma
        self.phase_reset()
        self.attn_scratch()
        QC = self.QC
        nd = QC // 128
        for h in range(4):
            kt = self.ktr.next()
            vh = self.vhr.next()
            dma('sp', kt.t[:, 0, :], self.KTM[h, 0], [self.db('KTM', (h, t)) for t in range(NT)], [kt])
            dma('sp', kt.t[0:64, 1, :], self.KTM[h, 1, 0:64, :], [self.db('KTM', (h, t)) for t in range(NT)], [kt])
            self.load_v(vh, h)

            def load_q(qc, h=h):
                qs = slice(qc * QC, (qc + 1) * QC)
                qt = self.qtr.next()
                gt = self.gtr.next()
                qdeps = [self.db('QTM', (h, qc * nd + i)) for i in range(nd)]
                dma('sp', qt.t[:, 0, :], self.QTM[h, 0, :, qs], qdeps, [qt])
                dma('sp', qt.t[0:64, 1, :], self.QTM[h, 1, 0:64, qs], qdeps, [qt])
                dma('sp', gt.t[:], self.GT[h * 128:(h + 1) * 128, qs], [self.db('GT', (h, qc * nd + i)) for i in range(nd)], [gt])
                return qt, gt

            def finalize(qc, os_, gt, h=h):
                qs = slice(qc * QC, (qc + 1) * QC)
                ob = self.obr.next()
                A('dve', 'tensor_tensor', [os_[0], gt], [ob], out=ob.t[:], in0=os_[0].t[:], in1=gt.t[:], op=ALU.mult)
                dma('sp', self.MIXT[h * 128:(h + 1) * 128, qs], ob.t[:], [ob], [self.db('MIXT', (h * 128, qc * nd + i)) for i in range(nd)])

            self.attention(h, kt, vh, True, 192.0 ** -0.5, load_q, finalize)

    Core.phaseB_odd = phaseB_odd


_phaseB_odd()


_NC_CACHE = {}


def kernel(**inputs):
    T = inputs['x'].shape[1]
    B = inputs['x'].shape[0]
    layers = (0, 1, 2, 3)
    if T not in _NC_CACHE:
        _NC_CACHE[T] = build(T, layers)
    nc = _NC_CACHE[T]
    inp = {k: np.asarray(v) for k, v in inputs.items()}
    maps = [prep_core_inputs(inp, b, T, layers) for b in range(B)]
    in_maps = [maps[c % B] for c in range(8)]
    res = run_bass_kernel_spmd(nc, in_maps, core_ids=list(range(8)))
    out = np.stack([np.asarray(res.results[b]["y"], dtype=np.float32) for b in range(B)], axis=0)
    return out
```
